# Optimizing a Trainium2 kernel written in Bass

```python
import jax, jax.numpy as jnp
from jax import lax
import numpy as np

D_MODEL = 1024
BATCH = 16
SEQ = 256
DEPTH = 4
DEC_BATCH = 4
DEC_SEQ = 2048
PAST_LEN = 512

GRID_W = 64
N_EVEN = (DEPTH + 1) // 2
N_ODD = DEPTH // 2
MIX_A = D_MODEL // 2
MIX_B = D_MODEL - MIX_A
GLA_HEADS = 4
GLA_DV = MIX_A // GLA_HEADS
GLA_DK = GLA_DV // 2
GLA_QK = GLA_HEADS * GLA_DK
GATE_RANK = 16
GATE_TAU = 16.0
CHUNK = 32
POOL_WINDOWS = (2, 4, 8, 16)
POOL_GROUPS = len(POOL_WINDOWS)
POOL_GW = MIX_B // POOL_GROUPS
MIX_C = D_MODEL // 2
MIX_D = D_MODEL - MIX_C
FOURIER_GROUPS = 4
FOURIER_GW = MIX_C // FOURIER_GROUPS
CONV_W = 3
D_FF = 4 * D_MODEL
EPS = 1e-6
EVEN_IN = 2 * GLA_QK + 2 * MIX_A + 2 * GATE_RANK + MIX_B
ODD_IN = MIX_C + 3 * MIX_D

kernel_name = 'hybrid_diffusion_gla_pool_fourier_conv_step'


def rmsnorm(x, g):
    xf = x.astype(jnp.float32)
    y = xf * lax.rsqrt(jnp.mean(xf * xf, axis=-1, keepdims=True) + EPS)
    return (y * g.astype(jnp.float32)).astype(x.dtype)


def grid_pos_embed(n_tok):
    rows = n_tok // GRID_W
    r = jnp.broadcast_to(jnp.arange(rows, dtype=jnp.float32)[:, None], (rows, GRID_W)).reshape(-1)
    col = jnp.broadcast_to(jnp.arange(GRID_W, dtype=jnp.float32)[None, :], (rows, GRID_W)).reshape(-1)
    quarter = D_MODEL // 4
    freqs = 1.0 / (10000.0 ** (jnp.arange(quarter, dtype=jnp.float32) / quarter))
    ar = r[:, None] * freqs
    ac = col[:, None] * freqs
    return jnp.concatenate([jnp.sin(ar), jnp.cos(ar), jnp.sin(ac), jnp.cos(ac)], axis=-1)


def gla_scan(q, k, v, logf, s0):
    B, L, H, DK = q.shape
    n = L // CHUNK

    def to_chunks(a):
        return jnp.moveaxis(a.reshape(B, n, CHUNK, *a.shape[2:]), 1, 0)

    mask = jnp.tril(jnp.ones((CHUNK, CHUNK), dtype=bool))

    def step(S, inp):
        qc, kc, vc, gc = inp
        b = jnp.cumsum(gc, axis=1)
        b_last = b[:, -1]
        o_inter = jnp.einsum('bihd,bhde->bihe', qc * jnp.exp(b), S)
        diff = b[:, :, None] - b[:, None, :]
        decay = jnp.exp(jnp.where(mask[None, :, :, None, None], diff, -jnp.inf))
        att = jnp.einsum('bihd,bjhd,bijhd->bhij', qc, kc, decay)
        o = o_inter + jnp.einsum('bhij,bjhe->bihe', att, vc)
        S_new = S * jnp.exp(b_last)[..., None] + jnp.einsum(
            'bjhd,bjhe->bhde', kc * jnp.exp(b_last[:, None] - b), vc)
        return S_new, o

    S_fin, o = lax.scan(step, s0, (to_chunks(q), to_chunks(k), to_chunks(v), to_chunks(logf)))
    o = jnp.moveaxis(o, 0, 1).reshape(B, L, H, v.shape[-1])
    return o, S_fin


def multiscale_pool(u, pool_w, pool_s):
    B, L, _ = u.shape
    uf = u.astype(jnp.float32)
    cs = jnp.concatenate([jnp.zeros((B, 1, MIX_B), jnp.float32), jnp.cumsum(uf, axis=1)], axis=1)
    t = jnp.arange(L)
    outs = []
    for gi, w in enumerate(POOL_WINDOWS):
        csg = cs[:, :, gi * POOL_GW:(gi + 1) * POOL_GW]
        lo = jnp.clip(t - w // 2, 0, L)
        hi = jnp.clip(t + w - w // 2, 0, L)
        cnt = (hi - lo).astype(jnp.float32)[None, :, None]
        outs.append((jnp.take(csg, hi, axis=1) - jnp.take(csg, lo, axis=1)) / cnt)
    pooled = (jnp.concatenate(outs, axis=-1) - uf).reshape(B, L, POOL_GROUPS, POOL_GW)
    y = jnp.einsum('blgc,gcd->blgd', pooled, pool_w.astype(jnp.float32)).reshape(B, L, MIX_B)
    return (y * pool_s.astype(jnp.float32)).astype(u.dtype)


def even_mixer(h, s0, w_in, w_a2, b_a2, gla_g, pool_w, pool_s, w_out):
    B, L, _ = h.shape
    f32 = jnp.float32
    p = h @ w_in
    o1 = GLA_QK
    o2 = o1 + GLA_QK
    o3 = o2 + MIX_A
    o4 = o3 + MIX_A
    o5 = o4 + 2 * GATE_RANK
    q = p[..., :o1].astype(f32).reshape(B, L, GLA_HEADS, GLA_DK) * (GLA_DK ** -0.5)
    k = p[..., o1:o2].astype(f32).reshape(B, L, GLA_HEADS, GLA_DK)
    v = p[..., o2:o3].astype(f32).reshape(B, L, GLA_HEADS, GLA_DV)
    g = p[..., o3:o4]
    glr = p[..., o4:o5].astype(f32).reshape(B, L, 2, GATE_RANK)
    u = p[..., o5:]
    pre = jnp.einsum('blzr,zrk->blzk', glr, w_a2.astype(f32)) + b_a2.astype(f32)
    logf = (jax.nn.log_sigmoid(pre) / GATE_TAU).reshape(B, L, 2, GLA_HEADS, GLA_DK)
    s0 = s0.astype(f32)
    o_f, s_f = gla_scan(q, k, v, logf[:, :, 0], s0[:, 0])
    o_b, s_b = gla_scan(jnp.flip(q, 1), jnp.flip(k, 1), jnp.flip(v, 1), jnp.flip(logf[:, :, 1], 1), s0[:, 1])
    o = o_f + jnp.flip(o_b, 1)
    o = o * lax.rsqrt(jnp.mean(o * o, axis=-1, keepdims=True) + EPS) * gla_g.astype(f32)
    o = o.reshape(B, L, MIX_A).astype(h.dtype) * jax.nn.silu(g)
    y_pool = multiscale_pool(u, pool_w, pool_s)
    y = jnp.concatenate([o, y_pool], axis=-1) @ w_out
    return y, jnp.stack([s_f, s_b], axis=1)


def odd_mixer(h, w_in, conv_w, conv_b, w_out):
    B, L, _ = h.shape
    p = h @ w_in
    f = p[..., :MIX_C]
    xin = p[..., MIX_C:MIX_C + MIX_D]
    bg = p[..., MIX_C + MIX_D:MIX_C + 2 * MIX_D]
    cg = p[..., MIX_C + 2 * MIX_D:]
    ff = f.astype(jnp.float32).reshape(B, L, FOURIER_GROUPS, FOURIER_GW)
    four = jnp.real(jnp.fft.fft2(ff, axes=(1, 3), norm='ortho')).reshape(B, L, MIX_C).astype(h.dtype)
    z = cg * xin
    pad = CONV_W // 2
    zp = jnp.pad(z, ((0, 0), (pad, CONV_W - 1 - pad), (0, 0)))
    conv = conv_b
    for j in range(CONV_W):
        conv = conv + zp[:, j:j + L] * conv_w[j]
    y_conv = bg * conv
    return jnp.concatenate([four, y_conv], axis=-1) @ w_out


def sq_relu_mlp(h, w1, w2):
    a = jax.nn.relu(h @ w1)
    return (a * a) @ w2


def trunk(x, cond, gla_init, norm_g, final_g, w_ada, b_ada, w_mlp1, w_mlp2,
          w_in_even, w_a2, b_a2, gla_norm_g, pool_w, pool_s, w_out_even,
          w_in_odd, conv_w, conv_b, w_out_odd):
    states = []
    for l in range(DEPTH):
        m = jax.nn.silu(cond) @ w_ada[l] + b_ada[l]
        sh1, sc1, g1, sh2, sc2, g2 = jnp.split(m, 6, axis=-1)
        h = rmsnorm(x, norm_g[l, 0]) * (1 + sc1) + sh1
        j = l // 2
        if l % 2 == 0:
            y, s = even_mixer(h, gla_init[:, j], w_in_even[j], w_a2[j], b_a2[j], gla_norm_g[j],
                              pool_w[j], pool_s[j], w_out_even[j])
            states.append(s)
        else:
            y = odd_mixer(h, w_in_odd[j], conv_w[j], conv_b[j], w_out_odd[j])
        x = x + g1 * y
        h = rmsnorm(x, norm_g[l, 1]) * (1 + sc2) + sh2
        x = x + g2 * sq_relu_mlp(h, w_mlp1[l], w_mlp2[l])
    return rmsnorm(x, final_g), jnp.stack(states, axis=1)


def setup_inputs(seed: int = 0) -> dict:
    key = jax.random.key(seed)
    ks = jax.random.split(key, 24)
    f32 = jnp.float32

    def nrm(k, shape, std):
        return jax.random.normal(k, shape, f32) * std

    return {
        'x_prompt': nrm(ks[0], (BATCH, SEQ, D_MODEL), 1.0),
        'x_sample': nrm(ks[1], (DEC_BATCH, DEC_SEQ, D_MODEL), 1.0),
        'state_gla': nrm(ks[2], (DEC_BATCH, N_EVEN, 2, GLA_HEADS, GLA_DK, GLA_DV), 2.0),
        'c': nrm(ks[3], (DEC_BATCH, D_MODEL), 1.0),
        'c_ctx': nrm(ks[4], (D_MODEL,), 1.0),
        'norm_g': 1.0 + nrm(ks[5], (DEPTH, 2, D_MODEL), 0.05),
        'final_g': 1.0 + nrm(ks[6], (D_MODEL,), 0.05),
        'w_ada': nrm(ks[7], (DEPTH, D_MODEL, 6 * D_MODEL), 0.5 * D_MODEL ** -0.5),
        'b_ada': nrm(ks[8], (DEPTH, 6 * D_MODEL), 0.02),
        'w_mlp1': nrm(ks[9], (DEPTH, D_MODEL, D_FF), D_MODEL ** -0.5),
        'w_mlp2': nrm(ks[10], (DEPTH, D_FF, D_MODEL), D_FF ** -0.5),
        'w_in_even': nrm(ks[11], (N_EVEN, D_MODEL, EVEN_IN), D_MODEL ** -0.5),
        'w_a2': nrm(ks[12], (N_EVEN, 2, GATE_RANK, GLA_QK), GATE_RANK ** -0.5),
        'b_a2': nrm(ks[13], (N_EVEN, 2, GLA_QK), 0.1),
        'gla_norm_g': 1.0 + nrm(ks[14], (N_EVEN, GLA_DV), 0.05),
        'pool_w': nrm(ks[15], (N_EVEN, POOL_GROUPS, POOL_GW, POOL_GW), POOL_GW ** -0.5),
        'pool_s': 1.0 + nrm(ks[16], (N_EVEN, MIX_B), 0.05),
        'w_out_even': nrm(ks[17], (N_EVEN, D_MODEL, D_MODEL), D_MODEL ** -0.5),
        'w_in_odd': nrm(ks[18], (N_ODD, D_MODEL, ODD_IN), D_MODEL ** -0.5),
        'conv_w': nrm(ks[19], (N_ODD, CONV_W, MIX_D), CONV_W ** -0.5),
        'conv_b': nrm(ks[20], (N_ODD, MIX_D), 0.02),
        'w_out_odd': nrm(ks[21], (N_ODD, D_MODEL, D_MODEL), D_MODEL ** -0.5),
    }


def reference(x_prompt, x_sample, state_gla, c, c_ctx, norm_g, final_g, w_ada, b_ada, w_mlp1, w_mlp2,
              w_in_even, w_a2, b_a2, gla_norm_g, pool_w, pool_s, w_out_even,
              w_in_odd, conv_w, conv_b, w_out_odd):
    zero_state = jnp.zeros((x_prompt.shape[0], N_EVEN, 2, GLA_HEADS, GLA_DK, GLA_DV), jnp.float32)
    y_prompt, new_state_gla = trunk(
        x_prompt, c_ctx[None, None, :], zero_state, norm_g, final_g, w_ada, b_ada, w_mlp1, w_mlp2,
        w_in_even, w_a2, b_a2, gla_norm_g, pool_w, pool_s, w_out_even, w_in_odd, conv_w, conv_b, w_out_odd)
    n_lat = x_sample.shape[1]
    x_lat = x_sample + grid_pos_embed(n_lat).astype(x_sample.dtype)[None]
    y_sample, _ = trunk(
        x_lat, c[:, None, :], state_gla, norm_g, final_g, w_ada, b_ada, w_mlp1, w_mlp2,
        w_in_even, w_a2, b_a2, gla_norm_g, pool_w, pool_s, w_out_even, w_in_odd, conv_w, conv_b, w_out_odd)
    return (y_prompt, y_sample, new_state_gla.astype(x_prompt.dtype))
```

```python
import numpy as np
import ml_dtypes
import concourse.bass as bass
import concourse.mybir as mybir
from concourse.bass_utils import run_bass_kernel_spmd

F32 = mybir.dt.float32
BF16 = mybir.dt.bfloat16
AF = mybir.ActivationFunctionType
ALU = mybir.AluOpType

D = 1024
KC = 8
NT = 2048
TT = 4
TW = 512
DEPTH = 4
DFF = 4096
EPS = 1e-6
NSLOT = 4
N_CORES = 8


class Res:
    __slots__ = ("name", "w", "r")

    def __init__(self, name):
        self.name = name
        self.w = None
        self.r = {}


class Prog:
    ENG = ("pe", "act", "dve", "pool", "sp")

    def __init__(self, nc, same_engine_sync=True):
        self.nc = nc
        self.eng = {"pe": nc.tensor, "act": nc.scalar, "dve": nc.vector,
                    "pool": nc.gpsimd, "sp": nc.sync}
        self.sems = {}
        self.cnt = {}
        for e in ("pe", "act", "dve", "pool"):
            self.sems[e] = nc.alloc_semaphore("c_" + e)
            self.cnt[e] = 0
        self.waited = {e: {} for e in self.ENG}
        self.same_engine_sync = same_engine_sync
        self.nwaits = 0
        self.ninst = {e: 0 for e in self.ENG}
        self.res = {}

    def R(self, name):
        r = self.res.get(name)
        if r is None:
            r = Res(name)
            self.res[name] = r
        return r

    def dma_sem(self, name):
        k = "d_" + name
        if k not in self.sems:
            self.sems[k] = self.nc.alloc_semaphore(k)
            self.cnt[k] = 0
        return k

    def _wait(self, e, tok):
        if tok is None:
            return
        k, v = tok
        if k == e and (e == "pe" or not self.same_engine_sync):
            return
        if self.waited[e].get(k, 0) >= v:
            return
        self.waited[e][k] = v
        self.eng[e].wait_ge(self.sems[k], v)
        self.nwaits += 1

    def _deps(self, e, reads, writes):
        for r in reads:
            self._wait(e, self.R(r).w)
        for r in writes:
            r = self.R(r)
            self._wait(e, r.w)
            for k, v in list(r.r.items()):
                self._wait(e, (k, v))

    def _commit(self, tok, reads, writes):
        k, v = tok
        for r in reads:
            r = self.R(r)
            if r.r.get(k, 0) < v:
                r.r[k] = v
        for r in writes:
            r = self.R(r)
            r.w = tok
            r.r = {}

    def op(self, e, fn, reads=(), writes=()):
        self._deps(e, reads, writes)
        ins = fn(self.eng[e])
        self.cnt[e] += 1
        ins.then_inc(self.sems[e], 1)
        self.ninst[e] += 1
        tok = (e, self.cnt[e])
        self._commit(tok, reads, writes)
        return tok

    def group(self, e, fns, reads=(), writes=()):
        self._deps(e, reads, writes)
        ins = None
        for fn in fns:
            ins = fn(self.eng[e])
            self.ninst[e] += 1
        self.cnt[e] += 1
        ins.then_inc(self.sems[e], 1)
        tok = (e, self.cnt[e])
        self._commit(tok, reads, writes)
        return tok

    def dma(self, q, semname, out, in_, reads=(), writes=(), **kw):
        k = self.dma_sem(semname)
        self._deps(q, reads, writes)
        if self.cnt[k] > 0:
            self._wait(q, (k, self.cnt[k]))
        ins = self.eng[q].dma_start(out=out, in_=in_, **kw)
        self.cnt[k] += 16
        ins.then_inc(self.sems[k], 16)
        self.ninst[q] += 1
        tok = (k, self.cnt[k])
        self._commit(tok, reads, writes)
        return tok

    def dma_batch(self, q, semname, items, **kw):
        k = self.dma_sem(semname)
        for (out, in_, reads, writes) in items:
            self._deps(q, reads, writes)
        for (out, in_, reads, writes) in items:
            if self.cnt[k] > 0:
                self._wait(q, (k, self.cnt[k]))
            ins = self.eng[q].dma_start(out=out, in_=in_, **kw)
            self.cnt[k] += 16
            ins.then_inc(self.sems[k], 16)
            self.ninst[q] += 1
        tok = (k, self.cnt[k])
        for (out, in_, reads, writes) in items:
            self._commit(tok, reads, writes)
        return tok

    def adopt(self, new_names, old_names):
        toks = {}
        for o in old_names:
            o = self.R(o)
            if o.w is not None:
                k, v = o.w
                toks[k] = max(toks.get(k, 0), v)
            for k, v in o.r.items():
                toks[k] = max(toks.get(k, 0), v)
        for n in new_names:
            n = self.R(n)
            for k, v in toks.items():
                if n.r.get(k, 0) < v:
                    n.r[k] = v

    def finish(self, e, names):
        for n in names:
            n = self.R(n)
            self._wait(e, n.w)
            for k, v in list(n.r.items()):
                self._wait(e, (k, v))


def mm(out, lhsT, rhs, start, stop):
    return lambda e: e.matmul(out, lhsT, rhs, start=start, stop=stop)


class Builder:
    def __init__(self, cfg):
        self.cfg = cfg
        nc = bass.Bass("TRN2", target_bir_lowering=False)
        self.nc = nc
        self.P = Prog(nc, same_engine_sync=cfg.get("ses", True))
        self.declare_dram()
        self.alloc()
        self.psum_i = 0
        self.wq = []
        self.wq_issued = 0
        self.wq_used = 0

    def declare_dram(self):
        nc = self.nc

        def din(name, shape, dt=F32):
            return nc.dram_tensor(name, list(shape), dt, kind="ExternalInput").ap()

        def dout(name, shape, dt=F32):
            return nc.dram_tensor(name, list(shape), dt, kind="ExternalOutput").ap()

        self.d_xT = din("xT", [D, NT])
        self.d_peRC = din("peRC", [128, 4 * 32 + 4 * 64])
        self.d_small = din("small", [128, 192])
        self.d_bada = din("b_ada", [DEPTH, 128, 48])
        self.d_wada = din("w_ada", [DEPTH, D, 6 * D])
        self.d_w1 = din("w_mlp1", [DEPTH, D, DFF])
        self.d_w2 = din("w_mlp2", [DEPTH, DFF, D])
        self.d_cb = din("cbf", [128, 768], BF16)
        self.d_cf = din("cf32", [128, 260])
        self.d_wio = din("w_in_odd", [2, D, 2048])
        self.d_woo = din("w_out_odd", [2, D, D])
        self.d_dft = din("dftm", [16, 128, 8, 512], BF16)
        self.d_wie = din("w_in_even", [2, D, 2080])
        self.d_woe = din("w_out_even", [2, D, D])
        self.d_wa2 = din("w_a2", [2, 2, 16, 256])
        self.d_ba2 = din("b_a2", [2, 512])
        self.d_poolw = din("pool_w", [2, 4, 128, 128])
        self.d_band = din("bandm", [128, 4096], BF16)
        self.d_s0 = din("s0", [2, 2, 128, 512])
        self.d_st = dout("st", [8, 2, 2, 128, 512])
        self.d_yT = dout("yT", [D, NT])

    def alloc(self):
        nc = self.nc
        A = lambda name, shape, dt: nc.alloc_sbuf_tensor("s_" + name, list(shape), dt).ap()
        self.x = A("x", [128, KC, NT], F32)
        self.hT = A("hT", [128, KC, NT], BF16)
        self.scr = A("scr", [128, 24576], BF16)
        self.wring = [A("wr%d" % i, [128, 4096], BF16) for i in range(NSLOT)]
        self.small = A("small", [128, 192], F32)
        self.cb = A("cb", [128, 768], BF16)
        self.cf = A("cf", [128, 260], F32)
        self.bada = A("bada", [128, DEPTH, 48], F32)
        self.mod = A("mod", [128, DEPTH, 48], F32)
        self.modA = A("modA", [128, DEPTH, 2, KC], F32)
        self.scb = A("scb", [128, KC], BF16)
        self.ss = A("ss", [128, 13312], BF16)
        self.sq = [self.carve(i * 2048, [128, 2, TW], BF16) for i in range(2)]
        self.rstd = [self.carve(4096 + i * 2048, [128, TW], F32) for i in range(2)]
        self.tmp = [self.carve(8192 + i * 2048, [128, TW], F32) for i in range(3)]
        self.rl = [self.carve(14336 + i * 1024, [128, TW], BF16) for i in range(3)]
        self.ss_names = ["sq0", "sq1", "rstd0", "rstd1", "tmp0", "tmp1", "tmp2", "rl0", "rl1", "rl2"]
        self.ps = [nc.alloc_psum_tensor("ps%d" % i, [128, TW], F32).ap() for i in range(8)]
        self.i_sq = self.i_rstd = self.i_tmp = self.i_rl = 0

    def carve(self, off, shape, dt, base=None, nparts=128):
        base = self.ss if base is None else base
        n = int(np.prod(shape[1:]))
        if dt == F32:
            v = base[0:nparts, off // 2: off // 2 + 2 * n].bitcast(F32)
        else:
            v = base[0:nparts, off // 2: off // 2 + n]
        if len(shape) == 2:
            return v
        names = "abcdef"[:len(shape) - 1]
        kw = {names[i]: shape[1 + i] for i in range(len(shape) - 2)}
        return v.rearrange("p (%s) -> p %s" % (" ".join(names), " ".join(names)), **kw)

    def bank(self):
        b = self.psum_i % 8
        self.psum_i += 1
        return b

    def wq_add(self, name, src_ap, view, hold=0):
        self.wq.append((name, src_ap, view, hold))

    def wq_pump(self, i):
        P = self.P
        while self.wq_issued < len(self.wq):
            k = self.wq_issued
            ev = k - NSLOT
            if ev >= 0 and ev + self.wq[ev][3] >= i:
                break
            name, src, view, hold = self.wq[k]
            s = k % NSLOT
            dst = self.wview(s, view)
            if isinstance(src, list):
                P.dma_batch("pool", "w%d" % s, [(sel(dst), sap, [], ["wslot%d" % s]) for sel, sap in src])
            else:
                P.dma("pool", "w%d" % s, dst, src, writes=["wslot%d" % s])
            self.wq_issued += 1

    def wq_get(self, name, ahead=None):
        i = self.wq_used
        n, src, view, hold = self.wq[i]
        assert n == name, (n, name)
        self.wq_pump(i)
        assert self.wq_issued > i
        self.wq_used += 1
        s = i % NSLOT
        return self.wview(s, view), "wslot%d" % s

    def wview(self, s, view):
        n = int(np.prod(view))
        names = "abcd"[:len(view)]
        kw = {names[i]: view[i] for i in range(len(view) - 1)}
        return self.wring[s][:, 0:n].rearrange("p (%s) -> p %s" % (" ".join(names), " ".join(names)), **kw)

    def plan_weights(self):
        cfg = self.cfg
        mlp_on = cfg.get("mlp", True)
        for l in range(cfg["depth"]):
            self.plan_mixer(l)
            for hb in range(4 if mlp_on else 0):
                for j in range(2):
                    c0 = hb * 1024 + j * 512
                    self.wq_add("w1_%d_%d_%d" % (l, hb, j),
                                self.d_w1[l].rearrange("(kc p) n -> p kc n", p=128)[:, :, c0:c0 + 512], (8, 512))
                for j in range(2):
                    r0 = hb * 1024 + j * 512
                    self.wq_add("w2_%d_%d_%d" % (l, hb, j),
                                self.d_w2[l, r0:r0 + 512, :].rearrange("(kc p) n -> p kc n", p=128), (4, 1024), hold=1 - j)

    def plan_mixer(self, l):
        if not self.cfg.get("mixer", True):
            return
        jl = l // 2
        if l % 2 == 1:
            if not self.cfg.get("odd", True):
                return
            wv = self.d_wio[jl].rearrange("(kc p) n -> p kc n", p=128)
            self.wq_add("wio_f_%d" % l, wv[:, :, 0:512], (8, 512))
            for i in range(16):
                self.wq_add("dft_%d_%d" % (l, i), self.d_dft[i], (8, 512))
            wo = self.d_woo[jl].rearrange("(kc p) n -> p kc n", p=128)
            self.wq_add("woo_A_%d" % l, wo[:, 0:4, :], (4, 1024))
            for ci in range(4):
                parts = []
                for blk in range(3):
                    c0 = 512 + blk * 512 + ci * 128
                    parts.append(((lambda blk: (lambda v: v[:, :, blk, :]))(blk), wv[:, :, c0:c0 + 128]))
                self.wq_add("wio_c_%d_%d" % (l, ci), parts, (8, 3, 128))
            self.wq_add("woo_B_%d" % l, wo[:, 4:8, :], (4, 1024))
        else:
            if not self.cfg.get("even", True):
                return
            self.plan_even(l)

    def plan_even(self, l):
        jl = l // 2
        wv = self.d_wie[jl].rearrange("(kc p) n -> p kc n", p=128)
        wo = self.d_woe[jl].rearrange("(kc p) n -> p kc n", p=128)
        self.wq_add("wie_u_%d" % l, wv[:, :, 1568:2080], (8, 512))
        self.wq_add("band_%d" % l, self.d_band.rearrange("p (g k t) -> p g k t", g=4, k=8), (4, 8, 128), hold=1)
        self.wq_add("poolw_%d" % l, self.d_poolw[jl].rearrange("g c d -> c g d"), (4, 128))
        self.wq_add("woe_A_%d" % l, wo[:, 4:8, :], (4, 1024))
        if self.cfg.get("estop", 9) <= 1:
            return
        self.wq_add("wie_glr_%d" % l, wv[:, :, 1536:1568], (8, 32))
        self.wq_add("wie_qk_%d" % l, wv[:, :, 0:512], (8, 512))
        self.wq_add("wie_v_%d" % l, wv[:, :, 512:1024], (8, 512))
        self.wq_add("wie_g_%d" % l, wv[:, :, 1024:1536], (8, 512))
        if self.cfg.get("estop", 9) <= 4:
            return
        self.wq_add("woe_B_%d" % l, wo[:, 0:4, :], (4, 1024))

    def prologue(self, after_tt=None):
        P, nc = self.P, self.nc
        P.dma("sp", "c0", self.small, self.d_small, writes=["small"])
        P.dma("sp", "c1", self.cb, self.d_cb, writes=["cb"])
        P.dma("sp", "c3", self.cf, self.d_cf, writes=["cf"])
        P.dma("sp", "c2", self.bada, self.d_bada.rearrange("l p c -> p l c"), writes=["bada"])
        self.cond = self.small[:, 0:8]
        self.keep = self.small[:, 8:9]
        self.ng = self.small[:, 16:80].rearrange("p (l s k) -> p l s k", l=DEPTH, s=2)
        self.fg = self.small[:, 80:88]
        self.ones_bf = self.cb[:, 0:128]
        self.ident_bf = self.cb[:, 128:256]
        P.op("act", lambda e: e.activation(self.scb, self.cond, AF.Silu), reads=["small"], writes=["scb"])
        self.ada_start(0)
        peRC = self.scr[:, 0:768].bitcast(F32)
        P.dma("sp", "pin0", peRC, self.d_peRC, writes=["scrpe_0"])
        peR = peRC[:, 0:128].rearrange("p (k r) -> p k r", k=4)
        peC = peRC[:, 128:384].rearrange("p (k c) -> p k c", k=4)
        xv = self.d_xT.rearrange("(kc p) t -> p kc t", p=128)
        for tt in range(TT):
            sl = slice(tt * TW, (tt + 1) * TW)
            xres = ["x_%d_%d" % (kc, tt) for kc in range(KC)]
            P.dma("sp", "xin%d" % tt, self.x[:, :, sl], xv[:, :, sl], writes=xres)
            xr_ = self.x[:, 0:4, sl].rearrange("p k (r c) -> p k r c", c=64)
            xc_ = self.x[:, 4:8, sl].rearrange("p k (r c) -> p k r c", c=64)
            P.op("dve", lambda e: e.tensor_tensor(xr_, xr_, peR[:, :, 8 * tt:8 * tt + 8].unsqueeze(3).broadcast_to([128, 4, 8, 64]), ALU.add),
                 reads=["scrpe_0"], writes=xres[0:4])
            P.op("dve", lambda e: e.tensor_tensor(xc_, xc_, peC.unsqueeze(2).broadcast_to([128, 4, 8, 64]), ALU.add),
                 reads=["scrpe_0"], writes=xres[4:8])
            if tt == 0:
                self.ada_step(0, 4)
            if after_tt is not None:
                after_tt(tt)
        P.adopt(["scr"], ["scrpe_0"])

    def mod_part(self, l, part):
        return self.mod[:, l, part * 8:(part + 1) * 8]

    def mod_res(self, l, part):
        return ["mod_%d_%d" % (l, part * 2), "mod_%d_%d" % (l, part * 2 + 1)]

    def prep_modA(self, l, s):
        P = self.P
        sc = self.mod_part(l, 1 + 3 * s)
        P.op("dve", lambda e: e.scalar_tensor_tensor(self.modA[:, l, s, :], sc, 1.0, self.ng[:, l, s, :], ALU.add, ALU.mult),
             reads=self.mod_res(l, 1 + 3 * s) + ["small"], writes=["modA_%d_%d" % (l, s)])

    def norm_begin(self, l, s, final=False):
        if not final:
            self.prep_modA(l, s)
            return dict(final=False, A=self.modA[:, l, s, :], B=self.mod_part(l, 3 * s),
                        dep=["modA_%d_%d" % (l, s)] + self.mod_res(l, 3 * s))
        return dict(final=True, A=self.fg, B=None, dep=["small"])

    def norm_tt(self, ctx, tt):
        P = self.P
        A, B, dep, final = ctx["A"], ctx["B"], ctx["dep"], ctx["final"]
        sl = slice(tt * TW, (tt + 1) * TW)
        b = self.bank()
        for k2 in range(KC // 2):
            sq = self.sq[self.i_sq % 2]
            sqn = "sq%d" % (self.i_sq % 2)
            self.i_sq += 1
            P.op("act", lambda e: e.activation(sq, self.x[:, 2 * k2:2 * k2 + 2, sl], AF.Square),
                 reads=["x_%d_%d" % (2 * k2, tt), "x_%d_%d" % (2 * k2 + 1, tt)], writes=[sqn])
            P.group("pe", [mm(self.ps[b], self.ones_bf, sq[:, j, :], k2 == 0 and j == 0, k2 == KC // 2 - 1 and j == 1) for j in range(2)],
                    reads=[sqn, "cb"], writes=["ps%d" % b])
        rs = self.rstd[self.i_rstd % 2]
        rsn = "rstd%d" % (self.i_rstd % 2)
        self.i_rstd += 1
        P.op("act", lambda e: e.activation(rs, self.ps[b], AF.Ln, bias=EPS), reads=["ps%d" % b], writes=[rsn])
        P.op("act", lambda e: e.activation(rs, rs, AF.Exp, scale=-0.5), reads=[rsn], writes=[rsn])
        for kc in range(KC):
            xr = "x_%d_%d" % (kc, tt)
            if final:
                P.op("dve", lambda e: e.scalar_tensor_tensor(self.x[:, kc, sl], self.x[:, kc, sl], A[:, kc:kc + 1], rs, ALU.mult, ALU.mult),
                     reads=[xr, rsn] + dep, writes=[xr])
                P.dma("sp", "yout%d" % kc, self.d_yT.rearrange("(kc p) t -> p kc t", p=128)[:, kc, sl], self.x[:, kc, sl], reads=[xr], writes=["yT_%d_%d" % (kc, tt)])
                continue
            t = self.tmp[self.i_tmp % 3]
            tn = "tmp%d" % (self.i_tmp % 3)
            self.i_tmp += 1
            P.op("dve", lambda e: e.scalar_tensor_tensor(t, self.x[:, kc, sl], A[:, kc:kc + 1], rs, ALU.mult, ALU.mult),
                 reads=[xr, rsn] + dep, writes=[tn])
            if True:
                P.op("act", lambda e: e.activation(self.hT[:, kc, sl], t, AF.Identity, bias=B[:, kc:kc + 1], scale=1.0),
                     reads=[tn] + dep, writes=["hT_%d_%d" % (kc, tt)])

    def norm(self, l, s, final=False):
        ctx = self.norm_begin(l, s, final)
        for tt in range(TT):
            self.norm_tt(ctx, tt)

    def mixer(self, l, after_tt=None):
        if l % 2 == 1 and self.cfg.get("odd", True):
            self.mixer_odd(l, after_tt)
        elif l % 2 == 0 and self.cfg.get("even", True):
            self.mixer_even(l, after_tt)
        elif after_tt is not None:
            for tt in range(TT):
                after_tt(tt)

    def cp(self, eng, out, in_):
        return (lambda e: e.copy(out, in_)) if eng == "act" else (lambda e: e.tensor_copy(out, in_))

    def mixer_even(self, l, after_tt=None):
        P = self.P
        nc = self.nc
        jl = l // 2
        scr, ps, hT, cb, cf, small = self.scr, self.ps, self.hT, self.cb, self.cf, self.small
        ones_bf, ident = self.ones_bf, self.ident_bf
        hT_all = ["hT_%d_%d" % (kc, tt) for kc in range(KC) for tt in range(TT)]
        ssn = []

        def C(name, off, shape, dt, nparts=128):
            ssn.append(name)
            return self.carve(off, shape, dt, nparts=nparts)
        glrT = C("glrT", 0, [64, NT], BF16, nparts=64)
        wa2b = C("wa2b", 4096, [64, 512], BF16, nparts=64)
        el = [C("el%d" % i, 5120 + 1024 * i, [128, 256], F32) for i in range(2)]
        Ep = C("Ep", 7168, [128, 256], F32)
        Em = C("Em", 8192, [128, 256], F32)
        qt = [C("qt%d" % i, 9216 + 512 * i, [128, 256], BF16) for i in range(2)]
        kt = [C("kt%d" % i, 10240 + 512 * i, [128, 256], BF16) for i in range(2)]
        qkT = [C("qkT%d" % i, 11264 + 1024 * i, [128, 4, 128], BF16) for i in range(2)]
        attm = [C("attm%d_0" % i, 13312 + 1024 * i, [128, 4, 128], BF16) for i in range(2)]
        ssn.extend(["attm0_1", "attm1_1", "otmp1"])
        S = C("S", 15360, [128, 512], F32)
        Sbf = C("Sbf", 17408, [128, 512], BF16)
        tmpS = C("tmpS", 18432, [128, 512], F32)
        otmp = C("otmp0", 20480, [128, 512], F32)
        sqo = C("sqo", 22528, [128, 512], BF16)
        rso = C("rso", 23552, [128, 512], F32)
        edec = [C("edec%d" % i, 25600 + 16 * i, [128, 2], F32) for i in range(2)]
        ptmp = [self.carve(20480 + 1024 * i, [128, 512], BF16) for i in range(2)]
        P.adopt(ssn + ["ptmp0", "ptmp1"], self.ss_names)
        pool_s = lambda g: small[:, 128 + jl * 4 + g: 128 + jl * 4 + g + 1]
        gla_g = small[:, 136 + jl: 137 + jl]

        utok = scr[:, 0:8192].rearrange("p (j c) -> p j c", j=16)
        ypT = scr[:, 8192:16384].rearrange("p (g t) -> p g t", g=4)
        ut_n = ["ut_%d" % j for j in range(16)]
        yp_n = ["yp_%d_%d" % (g, tt) for g in range(4) for tt in range(TT)]
        P.adopt(ut_n + yp_n, ["scr"])
        wu, wures = self.wq_get("wie_u_%d" % l)
        for j in range(16):
            b = self.bank()
            P.group("pe", [mm(ps[b], hT[:, kc, j * 128:(j + 1) * 128], wu[:, kc, :], kc == 0, kc == KC - 1) for kc in range(KC)],
                    reads=[wures] + ["hT_%d_%d" % (kc, j // 4) for kc in range(KC)], writes=["ps%d" % b])
            eng = "act" if j % 2 else "dve"
            P.op(eng, self.cp(eng, utok[:, j, :], ps[b]), reads=["ps%d" % b], writes=["ut_%d" % j])
        band, bandres = self.wq_get("band_%d" % l)
        pw, pwres = self.wq_get("poolw_%d" % l)
        KD = {"D_start": 0, "D_me": 1, "D_mo": 2, "D_end": 3, "L_same": 4, "L_cross": 5, "U_same": 6, "U_cross": 7}
        ip = 0
        for tt in range(TT):
            for g in range(4):
                b = self.bank()
                fns = []
                rd = set()
                for jj in range(4):
                    j = tt * 4 + jj
                    o = ps[b][:, jj * 128:(jj + 1) * 128]
                    srcs = []
                    if j > 0:
                        srcs.append((j - 1, KD["L_cross"] if j % 2 == 0 else KD["L_same"]))
                    srcs.append((j, KD["D_start"] if j == 0 else KD["D_end"] if j == 15 else KD["D_me"] if j % 2 == 0 else KD["D_mo"]))
                    if j < 15:
                        srcs.append((j + 1, KD["U_cross"] if j % 2 == 1 else KD["U_same"]))
                    for n_, (js, kind) in enumerate(srcs):
                        fns.append(mm(o, utok[:, js, g * 128:(g + 1) * 128], band[:, g, kind, :], n_ == 0, n_ == len(srcs) - 1))
                        rd.add("ut_%d" % js)
                P.group("pe", fns, reads=[bandres] + sorted(rd), writes=["ps%d" % b])
                pt = ptmp[ip % 2]
                ptn = "ptmp%d" % (ip % 2)
                ip += 1
                P.op("act", self.cp("act", pt, ps[b]), reads=["ps%d" % b], writes=[ptn])
                b2 = self.bank()
                P.group("pe", [mm(ps[b2], pw[:, g, :], pt, True, True)], reads=[pwres, ptn], writes=["ps%d" % b2])
                P.op("dve", (lambda b2, g, tt: (lambda e: e.tensor_scalar(ypT[:, g, tt * TW:(tt + 1) * TW], ps[b2], pool_s(g), None, ALU.mult)))(b2, g, tt),
                     reads=["ps%d" % b2, "small"], writes=["yp_%d_%d" % (g, tt)])
                if l == 0 and getattr(self, "_ada0_deferred", False) and tt < 2:
                    self.ada_step(0, 1)
        if l == 0 and getattr(self, "_ada0_deferred", False):
            self.ada_finish(0)
        if self.cfg.get("nopool"):
            self.wq_get("woe_A_%d" % l)
        else:
            self.out_proj(l, "woe_A_%d" % l, lambda kc, tt: ypT[:, kc, tt * TW:(tt + 1) * TW], lambda kc, tt: ["yp_%d_%d" % (kc, tt)])

        estop = self.cfg.get("estop", 9)
        if estop <= 1:
            P.adopt(["scr"], ut_n + yp_n)
            P.adopt(self.ss_names, ssn + ["ptmp0", "ptmp1"])
            return
        P.op("pool", lambda e: e.memset(glrT[32:64, :], 1.0), writes=["glrT"])
        P.op("pool", lambda e: e.memset(wa2b, 0.0), writes=["wa2b"])
        P.dma_batch("pool", "wa2", [
            (wa2b[0:16, 0:256], self.d_wa2[jl, 0], [], ["wa2b"]),
            (wa2b[16:32, 256:512], self.d_wa2[jl, 1], [], ["wa2b"]),
            (wa2b[32:33, :], self.d_ba2[jl:jl + 1, :], [], ["wa2b"])])
        wg_, wgres = self.wq_get("wie_glr_%d" % l)
        for tt in range(TT):
            sl = slice(tt * TW, (tt + 1) * TW)
            b = self.bank()
            P.group("pe", [mm(ps[b][0:32, :], wg_[:, kc, :], hT[:, kc, sl], kc == 0, kc == KC - 1) for kc in range(KC)],
                    reads=[wgres] + ["hT_%d_%d" % (kc, tt) for kc in range(KC)], writes=["ps%d" % b])
            P.op("act", self.cp("act", glrT[0:32, sl], ps[b][0:32, :]), reads=["ps%d" % b], writes=["glrT"])
        qkraw = scr[:, 0:8192].rearrange("p (j c) -> p j c", j=16)
        vtok = scr[:, 8192:16384].rearrange("p (j c) -> p j c", j=16)
        sgT = scr[:, 16384:24576].rearrange("p (h t) -> p h t", h=4)
        qk_n = ["qk_%d" % j for j in range(16)]
        v_n = ["v_%d" % j for j in range(16)]
        sg_n = ["sg_%d_%d" % (h, tt) for h in range(4) for tt in range(TT)]
        P.adopt(qk_n, ut_n)
        P.adopt(v_n, yp_n)
        P.adopt(sg_n, ["scr"])
        for nm, dst, rn in (("wie_qk_%d" % l, qkraw, "qk_%d"), ("wie_v_%d" % l, vtok, "v_%d")):
            w_, wres_ = self.wq_get(nm)
            for j in range(16):
                b = self.bank()
                P.group("pe", [mm(ps[b], hT[:, kc, j * 128:(j + 1) * 128], w_[:, kc, :], kc == 0, kc == KC - 1) for kc in range(KC)],
                        reads=[wres_] + ["hT_%d_%d" % (kc, j // 4) for kc in range(KC)], writes=["ps%d" % b])
                eng = "act" if j % 2 else "dve"
                P.op(eng, self.cp(eng, dst[:, j, :], ps[b]), reads=["ps%d" % b], writes=[rn % j])
        w_, wres_ = self.wq_get("wie_g_%d" % l)

        def g_group(n):
            h, tt = n // 4, n % 4
            sl = slice(tt * TW, (tt + 1) * TW)
            b = self.bank()
            P.group("pe", [mm(ps[b], w_[:, kc, h * 128:(h + 1) * 128], hT[:, kc, sl], kc == 0, kc == KC - 1) for kc in range(KC)],
                    reads=[wres_] + ["hT_%d_%d" % (kc, tt) for kc in range(KC)], writes=["ps%d" % b])
            P.op("act", lambda e: e.activation(sgT[:, h, sl], ps[b], AF.Silu), reads=["ps%d" % b], writes=["sg_%d_%d" % (h, tt)])
        for n in range(16):
            g_group(n)
        if estop <= 2:
            P.adopt(["scr"], qk_n + v_n + sg_n)
            P.adopt(self.ss_names, ssn + ["ptmp0", "ptmp1"])
            return
        if self.cfg.get("gla2", True):
            self.gla_v2(l, jl, glrT, wa2b, qkraw, vtok, sgT, ssn, hT_all, gla_g)
            P.adopt(self.ss_names, ssn + ["ptmp0", "ptmp1"])
            if self.cfg.get("nogla"):
                self.wq_get("woe_B_%d" % l)
                if after_tt is not None:
                    for tt in range(TT):
                        after_tt(tt)
            else:
                self.out_proj(l, "woe_B_%d" % l,
                              lambda kc, tt: qkraw[:, 4 * tt:4 * tt + 4, kc * 128:(kc + 1) * 128],
                              lambda kc, tt: ["qk_%d" % (4 * tt + i) for i in range(4)], after_tt=after_tt)
            P.adopt(["scr"], qk_n + v_n + sg_n)
            return
        ofT = hT[:, :, :].rearrange("p k t -> p (k t)").bitcast(F32).rearrange("p (h t) -> p h t", h=4)
        of_n = ["of_%d_%d" % (j, par) for j in range(16) for par in range(2)]
        P.adopt(of_n, hT_all)
        tri32 = [cf[:, 0:128], cf[:, 128:256]]
        onec = cf[:, 256:257]
        maskb = [cb[:, 512:640], cb[:, 640:768]]
        psb = lambda b: ps[b].bitcast(BF16)

        for z in range(2):
            order = list(range(16)) if z == 0 else list(range(15, -1, -1))
            P.dma("sp", "s0in", S, self.d_s0[jl, z], writes=["S"])
            P.op("act", self.cp("act", Sbf, S), reads=["S"], writes=["Sbf"])

            def stageA(n, j):
                r = n % 2
                b = self.bank()
                P.group("pe", [mm(ps[b][:, 0:256], glrT[0:33, j * 128:(j + 1) * 128], wa2b[0:33, z * 256:(z + 1) * 256], True, True)],
                        reads=["glrT", "wa2b"], writes=["ps%d" % b])
                e_ = el[r]
                en = "el%d" % r
                P.op("act", lambda e: e.activation(e_, ps[b][:, 0:256], AF.Exp, scale=-1.0), reads=["ps%d" % b], writes=[en])
                P.op("act", lambda e: e.activation(e_, e_, AF.Ln, bias=1.0), reads=[en], writes=[en])
                b2 = self.bank()
                P.group("pe", [mm(ps[b2][:, 0:256], tri32[z], e_, True, True),
                               mm(ps[b2][:, 256:257], e_[:, 0:128], onec, True, True),
                               mm(ps[b2][:, 257:258], e_[:, 128:256], onec, True, True)],
                        reads=[en, "cf"], writes=["ps%d" % b2])
                P.op("act", lambda e: e.activation(Ep, ps[b2][:, 0:256], AF.Exp, scale=-1.0 / 16.0), reads=["ps%d" % b2], writes=["Ep"])
                P.op("act", lambda e: e.activation(Em, ps[b2][:, 0:256], AF.Exp, scale=1.0 / 16.0), reads=["ps%d" % b2], writes=["Em"])
                P.op("act", lambda e: e.activation(edec[r], ps[b2][:, 256:258], AF.Exp, scale=-1.0 / 16.0), reads=["ps%d" % b2], writes=["edec%d" % r])
                P.op("dve", lambda e: e.scalar_tensor_tensor(qt[r], qkraw[:, j, 0:256], 0.125, Ep, ALU.mult, ALU.mult), reads=["qk_%d" % j, "Ep"], writes=["qt%d" % r])
                P.op("dve", lambda e: e.tensor_tensor(kt[r], qkraw[:, j, 256:512], Em, ALU.mult), reads=["qk_%d" % j, "Em"], writes=["kt%d" % r])
                b3 = self.bank()
                pb = psb(b3)
                fns = []
                for i4 in range(4):
                    src = (qt[r] if i4 < 2 else kt[r])[:, (i4 % 2) * 128:(i4 % 2 + 1) * 128]
                    fns.append((lambda i4, src: (lambda e: e.transpose(pb[:, i4 * 128:(i4 + 1) * 128], src, ident)))(i4, src))
                P.group("pe", fns, reads=["qt%d" % r, "kt%d" % r, "cb"], writes=["ps%d" % b3])
                P.op("act", self.cp("act", qkT[r], pb[:, 0:512].rearrange("p (a t) -> p a t", a=4)), reads=["ps%d" % b3], writes=["qkT%d" % r])

            def stageB(n, j):
                r = n % 2
                T_ = qkT[r]
                bA, bB = self.bank(), self.bank()
                hb_ = lambda h: (bA if h % 2 == 0 else bB)
                fns = []
                for h in range(4):
                    lo = (h % 2) * 64
                    fns.append(mm(ps[hb_(h)][:, (h // 2) * 128:(h // 2 + 1) * 128], T_[lo:lo + 64, 2 + h // 2, :], T_[lo:lo + 64, h // 2, :], True, True))
                P.group("pe", fns, reads=["qkT%d" % r], writes=["ps%d" % bA, "ps%d" % bB])
                am = attm[r]
                for par, bq in ((0, bA), (1, bB)):
                    P.op("dve", (lambda par, bq: (lambda e: e.tensor_tensor(am[:, par::2, :], ps[bq][:, 0:256].rearrange("p (h t) -> p h t", h=2),
                                                                            maskb[z].unsqueeze(1).broadcast_to([128, 2, 128]), ALU.mult)))(par, bq),
                         reads=["ps%d" % bq, "cb"], writes=["attm%d_%d" % (r, par)])
                oA, oB = self.bank(), self.bank()
                ob_ = lambda h: (oA if h % 2 == 0 else oB)
                fns = []
                for h in range(4):
                    lo = (h % 2) * 64
                    o = ps[ob_(h)][:, (h // 2) * 128:(h // 2 + 1) * 128]
                    fns.append(mm(o, vtok[:, j, h * 128:(h + 1) * 128], am[:, h, :], True, False))
                    fns.append(mm(o, Sbf[lo:lo + 64, (h // 2) * 256 + (h % 2) * 128:(h // 2) * 256 + (h % 2) * 128 + 128], T_[lo:lo + 64, h // 2, :], False, True))
                P.group("pe", fns, reads=["v_%d" % j, "attm%d_0" % r, "attm%d_1" % r, "Sbf", "qkT%d" % r], writes=["ps%d" % oA, "ps%d" % oB])
                b3 = self.bank()
                P.group("pe", [mm(ps[b3][:, hp * 256:(hp + 1) * 256], kt[r][:, hp * 128:(hp + 1) * 128], vtok[:, j, hp * 256:(hp + 1) * 256], True, True) for hp in range(2)],
                        reads=["kt%d" % r, "v_%d" % j], writes=["ps%d" % b3])
                P.op("dve", lambda e: e.tensor_tensor(tmpS, ps[b3], S, ALU.add), reads=["ps%d" % b3, "S"], writes=["tmpS"])
                seg_end = (j % 2 == 1) if z == 0 else (j % 2 == 0)
                edb = edec[r].unsqueeze(2).broadcast_to([128, 2, 256])
                t3 = tmpS.rearrange("p (a c) -> p a c", a=2)
                if seg_end:
                    P.op("dve", lambda e: e.tensor_tensor(t3, t3, edb, ALU.mult), reads=["tmpS", "edec%d" % r], writes=["tmpS"])
                    P.dma("sp", "stout", self.d_st[j // 2, jl, z], tmpS, reads=["tmpS"], writes=["st_%d_%d_%d" % (j // 2, jl, z)])
                    P.op("dve", lambda e: e.tensor_scalar(S, tmpS, self.keep, None, ALU.mult), reads=["tmpS", "small"], writes=["S"])
                else:
                    P.op("dve", lambda e: e.tensor_tensor(S.rearrange("p (a c) -> p a c", a=2), t3, edb, ALU.mult),
                         reads=["tmpS", "edec%d" % r], writes=["S"])
                P.op("act", self.cp("act", Sbf, S), reads=["S"], writes=["Sbf"])
                if z == 0:
                    for par, bq in ((0, oA), (1, oB)):
                        P.op("act", self.cp("act", ofT[:, par::2, j * 128:(j + 1) * 128], ps[bq][:, 0:256].rearrange("p (h t) -> p h t", h=2)),
                             reads=["ps%d" % bq], writes=["of_%d_%d" % (j, par)])
                else:
                    o4 = otmp.rearrange("p (h t) -> p h t", h=4)
                    for par, bq in ((0, oA), (1, oB)):
                        P.op("dve", (lambda par, bq: (lambda e: e.tensor_tensor(o4[:, par::2, :], ps[bq][:, 0:256].rearrange("p (h t) -> p h t", h=2),
                                                                                ofT[:, par::2, j * 128:(j + 1) * 128], ALU.add)))(par, bq),
                             reads=["ps%d" % bq, "of_%d_%d" % (j, par)], writes=["otmp%d" % par])
                    P.op("act", lambda e: e.activation(sqo, otmp, AF.Square), reads=["otmp0", "otmp1"], writes=["sqo"])
                    b4 = self.bank()
                    P.group("pe", [mm(ps[b4], ones_bf, sqo, True, True)], reads=["sqo", "cb"], writes=["ps%d" % b4])
                    P.op("act", lambda e: e.activation(rso, ps[b4], AF.Ln, bias=EPS, scale=8.0), reads=["ps%d" % b4], writes=["rso"])
                    P.op("act", lambda e: e.activation(rso, rso, AF.Exp, scale=-0.5), reads=["rso"], writes=["rso"])
                    P.op("dve", lambda e: e.scalar_tensor_tensor(otmp, otmp, gla_g, rso, ALU.mult, ALU.mult), reads=["otmp0", "otmp1", "rso", "small"], writes=["otmp0", "otmp1"])
                    P.op("dve", lambda e: e.tensor_tensor(qkraw[:, j, :].rearrange("p (h t) -> p h t", h=4), o4,
                                                          sgT[:, :, j * 128:(j + 1) * 128], ALU.mult),
                         reads=["otmp0", "otmp1"] + ["sg_%d_%d" % (h, j // 4) for h in range(4)], writes=["qk_%d" % j])

            stageA(0, order[0])
            for n in range(16):
                if n + 1 < 16:
                    stageA(n + 1, order[n + 1])
                if estop > 3:
                    stageB(n, order[n])
            if estop <= 4:
                break
        if estop <= 4:
            P.adopt(["scr"], qk_n + v_n + sg_n)
            P.adopt(hT_all, of_n)
            P.adopt(self.ss_names, ssn + ["ptmp0", "ptmp1"])
            return
        P.adopt(hT_all, of_n)
        P.adopt(self.ss_names, ssn + ["ptmp0", "ptmp1"])
        if self.cfg.get("nogla"):
            self.wq_get("woe_B_%d" % l)
            if after_tt is not None:
                for tt in range(TT):
                    after_tt(tt)
        else:
            self.out_proj(l, "woe_B_%d" % l,
                          lambda kc, tt: qkraw[:, 4 * tt:4 * tt + 4, kc * 128:(kc + 1) * 128],
                          lambda kc, tt: ["qk_%d" % (4 * tt + i) for i in range(4)], after_tt=after_tt)
        P.adopt(["scr"], qk_n + v_n + sg_n)

    def gla_v2(self, l, jl, glrT, wa2b, qkraw, vtok, sgT, ssn, hT_all, gla_g):
        P = self.P
        ps, hT, cb, cf, small = self.ps, self.hT, self.cb, self.cf, self.small
        ones_bf, ident = self.ones_bf, self.ident_bf
        tri32 = [cf[:, 0:128], cf[:, 128:256]]
        onec = cf[:, 256:257]
        mask2 = cb[:, 512:768].rearrange("p (z t) -> p z t", z=2)
        psb = lambda b: ps[b].bitcast(BF16)
        Sst = hT[:, :, :].rearrange("p k t -> p (k t)").rearrange("p (z j c) -> p z j c", z=2, j=16)
        st_n = ["Sst_%d_%d" % (z, j) for z in range(2) for j in range(16)]
        P.adopt(st_n, hT_all)

        def C(name, off, shape, dt):
            ssn.append(name)
            return self.carve(off, shape, dt)
        el_s = [C("g2_el%d" % i, 5120 + 2048 * i, [128, 512], F32) for i in range(2)]
        Em_s = [C("g2_Em%d" % i, 9216 + 2048 * i, [128, 512], F32) for i in range(2)]
        kt_s = [C("g2_kts%d" % i, 13312 + 1024 * i, [128, 512], BF16) for i in range(2)]
        S_ = [C("g2_S%d" % z, 15360 + 2048 * z, [128, 512], F32) for z in range(2)]
        tS_ = [C("g2_tS%d" % z, 19456 + 2048 * z, [128, 512], F32) for z in range(2)]
        ed_s = [C("g2_ed%d" % i, 23552 + 16 * i, [128, 4], F32) for i in range(2)]
        lb_s = [C("g2_lb%d" % i, 23584 + 1024 * i, [128, 512], BF16) for i in range(2)]
        tribf = [cb[:, 512:640], cb[:, 640:768]]
        passS_names = ["g2_lb0", "g2_lb1", "g2_el0", "g2_el1", "g2_Em0", "g2_Em1", "g2_kts0", "g2_kts1", "g2_S0", "g2_S1", "g2_tS0", "g2_tS1", "g2_ed0", "g2_ed1"]
        old_names = [n for n in ssn if n not in passS_names and n not in ("glrT", "wa2b")]
        P.adopt(passS_names, old_names)
        for z in range(2):
            P.dma("sp", "s0in%d" % z, S_[z], self.d_s0[jl, z], writes=["g2_S%d" % z])
            j0 = 0 if z == 0 else 15
            P.op("act", self.cp("act", Sst[:, z, j0, :], S_[z]), reads=["g2_S%d" % z], writes=["Sst_%d_%d" % (z, j0)])

        def make_S(n):
            r = n % 2
            jz = [n, 15 - n]
            el, Em, kt, ed, lb = el_s[r], Em_s[r], kt_s[r], ed_s[r], lb_s[r]
            eln, Emn, ktn, edn, lbn = "g2_el%d" % r, "g2_Em%d" % r, "g2_kts%d" % r, "g2_ed%d" % r, "g2_lb%d" % r
            st = {}

            def s0():
                b = st["b"] = self.bank()
                P.group("pe", [mm(ps[b][:, z * 256:(z + 1) * 256], glrT[0:33, jz[z] * 128:(jz[z] + 1) * 128], wa2b[0:33, z * 256:(z + 1) * 256], True, True) for z in range(2)],
                        reads=["glrT", "wa2b"], writes=["ps%d" % b])

            def s1():
                b = st["b"]
                P.op("act", lambda e: e.activation(el, ps[b], AF.Exp, scale=-1.0), reads=["ps%d" % b], writes=[eln])
                P.op("act", lambda e: e.activation(lb, el, AF.Ln, bias=1.0), reads=[eln], writes=[lbn])

            def s2():
                b2 = st["b2"] = self.bank()
                b3 = st["b3"] = self.bank()
                fns = [mm(ps[b2][:, z * 256:(z + 1) * 256], tribf[z], lb[:, z * 256:(z + 1) * 256], True, True) for z in range(2)]
                fns += [mm(ps[b3][:, i:i + 1], lb[:, i * 128:(i + 1) * 128], ones_bf[:, 0:1], True, True) for i in range(4)]
                P.group("pe", fns, reads=[lbn, "cb"], writes=["ps%d" % b2, "ps%d" % b3])

            def s3():
                b2, b3 = st["b2"], st["b3"]
                P.op("act", lambda e: e.activation(Em, ps[b2], AF.Exp, scale=1.0 / 16.0), reads=["ps%d" % b2], writes=[Emn])
                P.op("act", lambda e: e.activation(ed, ps[b3][:, 0:4], AF.Exp, scale=-1024.0 / 16.0), reads=["ps%d" % b3], writes=[edn])

            def s4():
                for z in range(2):
                    P.op("dve", lambda e: e.tensor_tensor(kt[:, z * 256:(z + 1) * 256], qkraw[:, jz[z], 256:512], Em[:, z * 256:(z + 1) * 256], ALU.mult),
                         reads=["qk_%d" % jz[z], Emn], writes=[ktn])

            def s5():
                for z in range(2):
                    j = jz[z]
                    bz = st["bz%d" % z] = self.bank()
                    P.group("pe", [mm(ps[bz][:, hp * 256:(hp + 1) * 256], kt[:, z * 256 + hp * 128:z * 256 + (hp + 1) * 128], vtok[:, j, hp * 256:(hp + 1) * 256], True, True) for hp in range(2)],
                            reads=[ktn, "v_%d" % j], writes=["ps%d" % bz])

            def s6():
                for z in range(2):
                    j = jz[z]
                    bz = st["bz%d" % z]
                    Sn, tn = "g2_S%d" % z, "g2_tS%d" % z
                    P.op("dve", lambda e: e.tensor_tensor(tS_[z], ps[bz], S_[z], ALU.add), reads=["ps%d" % bz, Sn], writes=[tn])
                    edb = ed[:, 2 * z:2 * z + 2].unsqueeze(2).broadcast_to([128, 2, 256])
                    t3 = tS_[z].rearrange("p (a c) -> p a c", a=2)
                    seg_end = (j % 2 == 1) if z == 0 else (j % 2 == 0)
                    if seg_end:
                        P.op("dve", lambda e: e.tensor_tensor(t3, t3, edb, ALU.mult), reads=[tn, edn], writes=[tn])
                        P.dma("sp", "stout%d" % z, self.d_st[j // 2, jl, z], tS_[z], reads=[tn], writes=["st_%d_%d_%d" % (j // 2, jl, z)])
                        P.op("dve", lambda e: e.tensor_scalar(S_[z], tS_[z], self.keep, None, ALU.mult), reads=[tn, "small"], writes=[Sn])
                    else:
                        P.op("dve", lambda e: e.tensor_tensor(S_[z].rearrange("p (a c) -> p a c", a=2), t3, edb, ALU.mult), reads=[tn, edn], writes=[Sn])
                    jn = j + 1 if z == 0 else j - 1
                    if 0 <= jn <= 15:
                        P.op("act", self.cp("act", Sst[:, z, jn, :], S_[z]), reads=[Sn], writes=["Sst_%d_%d" % (z, jn)])
            return [s0, s1, s2, s3, s4, s5, s6]

        def pipeline(levels_of, n_items, split):
            items = [levels_of(i) for i in range(n_items)]
            nl = len(items[0])
            for i in range(n_items + 1):
                for k in range(max(split, nl - split)):
                    if i < n_items and k < split:
                        items[i][k]()
                    if i >= 1 and split + k < nl:
                        items[i - 1][split + k]()

        pipeline(make_S, 16, 4)
        X_o = [C("g2o_X%d" % i, 5120 + 2048 * i, [128, 512], F32) for i in range(2)]
        Y_o = [C("g2o_Y%d" % i, 9216 + 2048 * i, [128, 512], F32) for i in range(2)]
        qt_o = [C("g2o_qt%d" % i, 13312 + 1024 * i, [128, 512], BF16) for i in range(2)]
        kt_o = [C("g2o_kt%d" % i, 15360 + 1024 * i, [128, 512], BF16) for i in range(2)]
        qkT_o = [C("g2o_qkT%d" % i, 17408 + 2048 * i, [128, 8, 128], BF16) for i in range(2)]
        attm_o = [C("g2o_attm%d_0" % i, 21504 + 2048 * i, [128, 8, 128], BF16) for i in range(2)]
        ssn.extend(["g2o_attm0_1", "g2o_attm1_1"])
        ssn.extend(["g2o_qkT0k", "g2o_qkT1k"])
        passO_names = ["g2o_X0", "g2o_X1", "g2o_Y0", "g2o_Y1", "g2o_qt0", "g2o_qt1", "g2o_kt0", "g2o_kt1", "g2o_qkT0", "g2o_qkT1", "g2o_qkT0k", "g2o_qkT1k",
                       "g2o_attm0_0", "g2o_attm0_1", "g2o_attm1_0", "g2o_attm1_1"]
        P.adopt(passO_names, passS_names)
        order = list(range(15, -1, -1))

        def make_O(n):
            j = order[n]
            r = n % 2
            X, Y, qt, kt, T_, attm = X_o[r], Y_o[r], qt_o[r], kt_o[r], qkT_o[r], attm_o[r]
            Xn, Yn, qtn, ktn, Tn = "g2o_X%d" % r, "g2o_Y%d" % r, "g2o_qt%d" % r, "g2o_kt%d" % r, "g2o_qkT%d" % r
            amn = ["g2o_attm%d_0" % r, "g2o_attm%d_1" % r]
            st = {}

            def o0():
                b = st["b"] = self.bank()
                P.group("pe", [mm(ps[b], glrT[0:33, j * 128:(j + 1) * 128], wa2b[0:33, :], True, True)], reads=["glrT", "wa2b"], writes=["ps%d" % b])

            def o1():
                b = st["b"]
                P.op("act", lambda e: e.activation(X, ps[b], AF.Exp, scale=-1.0), reads=["ps%d" % b], writes=[Xn])
                P.op("act", lambda e: e.activation(qt, X, AF.Ln, bias=1.0), reads=[Xn], writes=[qtn])

            def o2():
                b2 = st["b2"] = self.bank()
                P.group("pe", [mm(ps[b2][:, z * 256:(z + 1) * 256], tribf[z], qt[:, z * 256:(z + 1) * 256], True, True) for z in range(2)],
                        reads=[qtn, "cb"], writes=["ps%d" % b2])

            def o3():
                b2 = st["b2"]
                P.op("act", lambda e: e.activation(X, ps[b2], AF.Exp, scale=-1.0 / 16.0), reads=["ps%d" % b2], writes=[Xn])
                P.op("act", lambda e: e.activation(Y, ps[b2], AF.Exp, scale=1.0 / 16.0), reads=["ps%d" % b2], writes=[Yn])

            def o4():
                q2 = qkraw[:, j, 0:256].unsqueeze(1).broadcast_to([128, 2, 256])
                k2 = qkraw[:, j, 256:512].unsqueeze(1).broadcast_to([128, 2, 256])
                P.op("dve", lambda e: e.scalar_tensor_tensor(qt.rearrange("p (z c) -> p z c", z=2), q2, 0.125, X.rearrange("p (z c) -> p z c", z=2), ALU.mult, ALU.mult),
                     reads=["qk_%d" % j, Xn], writes=[qtn])
                P.op("dve", lambda e: e.tensor_tensor(kt.rearrange("p (z c) -> p z c", z=2), k2, Y.rearrange("p (z c) -> p z c", z=2), ALU.mult),
                     reads=["qk_%d" % j, Yn], writes=[ktn])

            def o5():
                b3 = st["b3"] = self.bank()
                pb = psb(b3)
                fns = []
                for i8 in range(8):
                    src = (qt if i8 < 4 else kt)[:, (i8 % 4) * 128:(i8 % 4 + 1) * 128]
                    fns.append((lambda i8, src: (lambda e: e.transpose(pb[:, i8 * 128:(i8 + 1) * 128], src, ident)))(i8, src))
                P.group("pe", fns, reads=[qtn, ktn, "cb"], writes=["ps%d" % b3])

            def o6():
                b3 = st["b3"]
                P.op("act", self.cp("act", T_, psb(b3).rearrange("p (a t) -> p a t", a=8)), reads=["ps%d" % b3], writes=[Tn])

            def o7():
                bA, bB = st["bA"], st["bB"] = self.bank(), self.bank()
                fns = []
                for z in range(2):
                    for h in range(4):
                        lo = (h % 2) * 64
                        bq = bA if h % 2 == 0 else bB
                        c0 = (z * 2 + h // 2) * 128
                        fns.append(mm(ps[bq][:, c0:c0 + 128], T_[lo:lo + 64, 4 + z * 2 + h // 2, :], T_[lo:lo + 64, z * 2 + h // 2, :], True, True))
                P.group("pe", fns, reads=[Tn], writes=["ps%d" % bA, "ps%d" % bB])

            def o8():
                am5 = attm.rearrange("p (z hh par) t -> p z hh par t", z=2, hh=2)
                for par, bq in ((0, st["bA"]), (1, st["bB"])):
                    P.op("dve", lambda e: e.tensor_tensor(am5[:, :, :, par, :], ps[bq].rearrange("p (z hh t) -> p z hh t", z=2, hh=2),
                                                          mask2.unsqueeze(2).broadcast_to([128, 2, 2, 128]), ALU.mult),
                         reads=["ps%d" % bq, "cb"], writes=[amn[par]])

            def o9():
                oA, oB = st["oA"], st["oB"] = self.bank(), self.bank()
                fns = []
                for h in range(4):
                    lo = (h % 2) * 64
                    o = ps[oA if h % 2 == 0 else oB][:, (h // 2) * 128:(h // 2 + 1) * 128]
                    sc0 = (h // 2) * 256 + (h % 2) * 128
                    for z in range(2):
                        fns.append(mm(o, vtok[:, j, h * 128:(h + 1) * 128], attm[:, z * 4 + h, :], z == 0, False))
                        fns.append(mm(o, Sst[lo:lo + 64, z, j, sc0:sc0 + 128], T_[lo:lo + 64, z * 2 + h // 2, :], False, z == 1))
                P.group("pe", fns, reads=["v_%d" % j, amn[0], amn[1], "Sst_0_%d" % j, "Sst_1_%d" % j, Tn], writes=["ps%d" % oA, "ps%d" % oB])

            def o10():
                sq4 = qt.rearrange("p (h t) -> p h t", h=4)
                for par, bq in ((0, st["oA"]), (1, st["oB"])):
                    P.op("act", lambda e: e.activation(sq4[:, par::2, :], ps[bq][:, 0:256].rearrange("p (h t) -> p h t", h=2), AF.Square),
                         reads=["ps%d" % bq], writes=[qtn])

            def o11():
                b4 = st["b4"] = self.bank()
                P.group("pe", [mm(ps[b4], ones_bf, qt, True, True)], reads=[qtn, "cb"], writes=["ps%d" % b4])

            def o12():
                b4 = st["b4"]
                P.op("act", lambda e: e.activation(X, ps[b4], AF.Ln, bias=EPS, scale=8.0), reads=["ps%d" % b4], writes=[Xn])
                P.op("act", lambda e: e.activation(X, X, AF.Exp, scale=-0.5), reads=[Xn], writes=[Xn])

            def o13():
                r4 = X.rearrange("p (h t) -> p h t", h=4)
                o4 = Y.rearrange("p (h t) -> p h t", h=4)
                for par, bq in ((0, st["oA"]), (1, st["oB"])):
                    P.op("dve", lambda e: e.scalar_tensor_tensor(o4[:, par::2, :], ps[bq][:, 0:256].rearrange("p (h t) -> p h t", h=2), gla_g, r4[:, par::2, :], ALU.mult, ALU.mult),
                         reads=["ps%d" % bq, Xn, "small"], writes=[Yn])
                P.op("dve", lambda e: e.tensor_tensor(qkraw[:, j, :].rearrange("p (h t) -> p h t", h=4), o4, sgT[:, :, j * 128:(j + 1) * 128], ALU.mult),
                     reads=[Yn] + ["sg_%d_%d" % (h, j // 4) for h in range(4)], writes=["qk_%d" % j])
            return [o0, o1, o2, o3, o4, o5, o6, o7, o8, o9, o10, o11, o12, o13]

        pipeline(make_O, 16, 7)
        P.adopt(hT_all, st_n)

    def out_proj(self, l, wname, src, srcres, after_tt=None):
        P = self.P
        g1 = self.mod_part(l, 2)
        g1dep = self.mod_res(l, 2)
        w, wres = self.wq_get(wname)
        for tt in range(TT):
            sl = slice(tt * TW, (tt + 1) * TW)
            for m in range(KC):
                b = self.bank()
                P.group("pe", [mm(self.ps[b], w[:, kc, m * 128:(m + 1) * 128], src(kc, tt), kc == 0, kc == 3) for kc in range(4)],
                        reads=[wres] + [r for kc in range(4) for r in srcres(kc, tt)], writes=["ps%d" % b])
                xr = "x_%d_%d" % (m, tt)
                P.op("dve", lambda e: e.scalar_tensor_tensor(self.x[:, m, sl], self.ps[b], g1[:, m:m + 1], self.x[:, m, sl], ALU.mult, ALU.add),
                     reads=["ps%d" % b, xr] + g1dep, writes=[xr])
            if after_tt is not None:
                after_tt(tt)

    def mixer_odd(self, l, after_tt=None):
        P = self.P
        jl = l // 2
        scr = self.scr
        ps = self.ps
        hT = self.hT
        fT = scr[:, 0:8192].rearrange("p (g t) -> p g t", g=4)
        AT = scr[:, 8192:24576].rearrange("p (lt g c) -> p lt g c", lt=16, g=4)
        fT_n = ["fT_%d_%d" % (g, tt) for g in range(4) for tt in range(TT)]
        AT_n = ["AT_%d_%d" % (lt, gp) for lt in range(16) for gp in range(2)]
        P.adopt(fT_n + AT_n, ["scr"])
        CS = self.cb[:, 256:512]
        w, wres = self.wq_get("wio_f_%d" % l)
        for tt in range(TT):
            for g in range(4):
                sl = slice(tt * TW, (tt + 1) * TW)
                b = self.bank()
                P.group("pe", [mm(ps[b], w[:, kc, g * 128:(g + 1) * 128], hT[:, kc, sl], kc == 0, kc == KC - 1) for kc in range(KC)],
                        reads=[wres] + ["hT_%d_%d" % (kc, tt) for kc in range(KC)], writes=["ps%d" % b])
                P.op("act", (lambda b, g, sl: (lambda e: e.copy(fT[:, g, sl], ps[b])))(b, g, sl), reads=["ps%d" % b], writes=["fT_%d_%d" % (g, tt)])
        for lt in range(16):
            for gp in range(2):
                b = self.bank()
                P.group("pe", [mm(ps[b][:, gg * 256:(gg + 1) * 256], fT[:, gp * 2 + gg, lt * 128:(lt + 1) * 128], CS, True, True) for gg in range(2)],
                        reads=["cb"] + ["fT_%d_%d" % (gp * 2 + gg, lt // 4) for gg in range(2)], writes=["ps%d" % b])
                P.op("dve" if (lt + gp) % 2 else "act",
                     (lambda b, lt, gp: (lambda e: (e.copy if e is self.nc.scalar else e.tensor_copy)(AT[:, lt, gp * 2:gp * 2 + 2, :], ps[b].rearrange("p (g c) -> p g c", g=2))))(b, lt, gp),
                     reads=["ps%d" % b], writes=["AT_%d_%d" % (lt, gp)])
        fourT = fT
        four_n = ["four_%d_%d" % (g, tq) for g in range(4) for tq in range(TT)]
        P.adopt(four_n, fT_n)
        for tq in range(TT):
            bg = [self.bank() for g in range(4)]
            for pi in range(4):
                t, half = pi // 2, pi % 2
                wp, wpres = self.wq_get("dft_%d_%d" % (l, tq * 4 + pi))
                for g in range(4):
                    fns = [mm(ps[bg[g]], AT[:, half * 8 + i, g, t * 128:(t + 1) * 128], wp[:, i, :], pi == 0 and i == 0, pi == 3 and i == 7) for i in range(8)]
                    P.group("pe", fns, reads=[wpres] + ["AT_%d_%d" % (half * 8 + i, g // 2) for i in range(8)], writes=["ps%d" % bg[g]])
            for g in range(4):
                eng = "act" if g % 2 else "dve"
                P.op(eng, self.cp(eng, fourT[:, g, tq * TW:(tq + 1) * TW], ps[bg[g]]), reads=["ps%d" % bg[g]], writes=["four_%d_%d" % (g, tq)])
        self.out_proj(l, "woo_A_%d" % l, lambda kc, tt: fourT[:, kc, tt * TW:(tt + 1) * TW],
                      lambda kc, tt: ["four_%d_%d" % (kc, tt)])
        ycT = fourT
        yc_n = ["yc_%d" % ci for ci in range(4)]
        P.adopt(yc_n, four_n)
        zp = self.carve(16384, [128, 8, 258], F32, base=scr)
        cbuf = self.carve(16384 + 8448, [128, 8, 256], F32, base=scr)
        bgb = self.carve(16384 + 8448 + 8192, [128, 8, 256], BF16, base=scr)
        P.adopt(["zp", "cbuf", "bgb"], AT_n)
        for ci in range(4):
            wci, wcires = self.wq_get("wio_c_%d_%d" % (l, ci))
            P.op("pool", lambda e: e.memset(zp[:, :, 0:258:257], 0.0), writes=["zp"])
            for tt in range(TT):
                sl = slice(tt * TW, (tt + 1) * TW)
                bx, bb, bc = self.bank(), self.bank(), self.bank()
                for blk, b in ((0, bx), (1, bb), (2, bc)):
                    P.group("pe", [mm(ps[b], wci[:, kc, blk, :], hT[:, kc, sl], kc == 0, kc == KC - 1) for kc in range(KC)],
                            reads=[wcires] + ["hT_%d_%d" % (kc, tt) for kc in range(KC)], writes=["ps%d" % b])
                t = self.tmp[self.i_tmp % 3]
                tn = "tmp%d" % (self.i_tmp % 3)
                self.i_tmp += 1
                P.op("act", (lambda t, bx: (lambda e: e.copy(t, ps[bx])))(t, bx), reads=["ps%d" % bx], writes=[tn])
                P.op("act", (lambda bb, tt: (lambda e: e.copy(bgb[:, 2 * tt:2 * tt + 2, :], ps[bb].rearrange("p (s c) -> p s c", s=2))))(bb, tt),
                     reads=["ps%d" % bb], writes=["bgb"])
                P.op("dve", (lambda t, bc, tt: (lambda e: e.tensor_tensor(zp[:, 2 * tt:2 * tt + 2, 1:257], ps[bc].rearrange("p (s c) -> p s c", s=2),
                                                                          t.rearrange("p (s c) -> p s c", s=2), ALU.mult)))(t, bc, tt),
                     reads=["ps%d" % bc, tn], writes=["zp"])
            P.op("dve", lambda e: e.tensor_scalar(zp[:, 1:8, 0:1], zp[:, 0:7, 256:257], self.keep, None, ALU.mult), reads=["zp", "small"], writes=["zp"])
            P.op("dve", lambda e: e.tensor_scalar(zp[:, 0:7, 257:258], zp[:, 1:8, 1:2], self.keep, None, ALU.mult), reads=["zp", "small"], writes=["zp"])
            cw = lambda k: self.small[:, 96 + (jl * 3 + k) * 4 + ci: 96 + (jl * 3 + k) * 4 + ci + 1]
            cbias = self.small[:, 120 + jl * 4 + ci: 120 + jl * 4 + ci + 1]
            P.op("dve", (lambda ci, cw, cbias: (lambda e: e.tensor_scalar(cbuf, zp[:, :, 1:257], cw(1), cbias, ALU.mult, ALU.add)))(ci, cw, cbias),
                 reads=["zp", "small"], writes=["cbuf"])
            P.op("dve", (lambda ci, cw: (lambda e: e.scalar_tensor_tensor(cbuf, zp[:, :, 0:256], cw(0), cbuf, ALU.mult, ALU.add)))(ci, cw),
                 reads=["zp", "small", "cbuf"], writes=["cbuf"])
            P.op("dve", (lambda ci, cw: (lambda e: e.scalar_tensor_tensor(cbuf, zp[:, :, 2:258], cw(2), cbuf, ALU.mult, ALU.add)))(ci, cw),
                 reads=["zp", "small", "cbuf"], writes=["cbuf"])
            P.op("dve", (lambda ci: (lambda e: e.tensor_tensor(ycT[:, ci, :].rearrange("p (s c) -> p s c", s=8), bgb, cbuf, ALU.mult)))(ci),
                 reads=["bgb", "cbuf"], writes=["yc_%d" % ci])
        self.out_proj(l, "woo_B_%d" % l, lambda kc, tt: ycT[:, kc, tt * TW:(tt + 1) * TW], lambda kc, tt: ["yc_%d" % kc], after_tt=after_tt)
        P.adopt(["scr"], yc_n + ["zp", "cbuf", "bgb"])

    def build(self):
        cfg = self.cfg
        P = self.P
        depth = cfg["depth"]
        self.plan_weights()
        mlp_on = cfg.get("mlp", True)
        mix_on = cfg.get("mixer", True)
        overlap = cfg.get("overlap", True) and mlp_on and mix_on
        self._ada0_deferred = mix_on and cfg.get("even", True)

        def run_all(ctx):
            for tt in range(TT):
                self.norm_tt(ctx, tt)

        def lazy_hook(make_ctx):
            box = {}

            def h(tt):
                if "c" not in box:
                    box["c"] = make_ctx()
                self.norm_tt(box["c"], tt)
            return h

        fin = lambda: self.norm_begin(0, 0, final=True)
        self.prologue(after_tt=lazy_hook(lambda: self.norm_begin(0, 0)) if overlap else None)
        if not self._ada0_deferred:
            self.ada_finish(0)
        if not mlp_on:
            for l2 in range(1, depth):
                self.ada_start(l2)
                self.ada_finish(l2)
        if overlap:
            for l in range(depth):
                last = l + 1 == depth
                self.mixer(l, after_tt=lazy_hook((lambda l: (lambda: self.norm_begin(l, 1)))(l)))
                self.mlp(l, after_tt=lazy_hook(fin if last else (lambda l: (lambda: self.norm_begin(l + 1, 0)))(l)))
        else:
            for l in range(depth):
                if mix_on:
                    run_all(self.norm_begin(l, 0))
                    self.mixer(l)
                if mlp_on:
                    run_all(self.norm_begin(l, 1))
                    self.mlp(l)
            run_all(fin())
        P.finish("sp", ["yT_%d_%d" % (kc, tt) for kc in range(KC) for tt in range(TT)])
        P.finish("sp", [n for n in P.res if n.startswith("st_")])
        return self.nc


def mlp(self, l, after_tt=None):
    P = self.P
    g2 = self.mod_part(l, 5)
    g2dep = self.mod_res(l, 5)
    a = self.scr[:, 0:16384].rearrange("p (c t) -> p c t", c=8)
    an = "aT"
    aT_n = [an + "_%d_%d" % (c, t) for c in range(8) for t in range(TT)]
    P.adopt(aT_n, ["scr"])
    nxt = l + 1 if l + 1 < self.cfg["depth"] else None
    if nxt is not None:
        self.ada_start(nxt)
    for hb in range(4):
        for j in range(2):
            w1, w1res = self.wq_get("w1_%d_%d_%d" % (l, hb, j))
            if nxt is not None and hb < 3:
                self.ada_step(nxt, 2)
            for tt in range(TT):
                sl = slice(tt * TW, (tt + 1) * TW)
                for c4 in range(4):
                    c = j * 4 + c4
                    b = self.bank()
                    P.group("pe", [mm(self.ps[b], w1[:, kc, c4 * 128:(c4 + 1) * 128], self.hT[:, kc, sl], kc == 0, kc == KC - 1) for kc in range(KC)],
                            reads=[w1res] + ["hT_%d_%d" % (kc, tt) for kc in range(KC)], writes=["ps%d" % b])
                    r = self.rl[self.i_rl % 3]
                    rn = "rl%d" % (self.i_rl % 3)
                    self.i_rl += 1
                    P.op("act", lambda e: e.activation(r, self.ps[b], AF.Relu), reads=["ps%d" % b], writes=[rn])
                    P.op("dve", lambda e: e.tensor_tensor(a[:, c, sl], self.ps[b], r, ALU.mult),
                         reads=["ps%d" % b, rn], writes=[an + "_%d_%d" % (c, tt)])
        w2a, w2ares = self.wq_get("w2_%d_%d_0" % (l, hb))
        w2b, w2bres = self.wq_get("w2_%d_%d_1" % (l, hb))
        for tt in range(TT):
            sl = slice(tt * TW, (tt + 1) * TW)
            for m in range(KC):
                b = self.bank()
                fns = []
                for c in range(8):
                    w2 = w2a if c < 4 else w2b
                    fns.append(mm(self.ps[b], w2[:, c % 4, m * 128:(m + 1) * 128], a[:, c, sl], c == 0, c == 7))
                P.group("pe", fns, reads=[w2ares, w2bres] + [an + "_%d_%d" % (c, tt) for c in range(8)], writes=["ps%d" % b])
                xr = "x_%d_%d" % (m, tt)
                P.op("dve", lambda e: e.scalar_tensor_tensor(self.x[:, m, sl], self.ps[b], g2[:, m:m + 1], self.x[:, m, sl], ALU.mult, ALU.add),
                     reads=["ps%d" % b, xr] + g2dep, writes=[xr])
            if hb == 3 and after_tt is not None:
                after_tt(tt)
    if nxt is not None:
        self.ada_finish(nxt)
    P.adopt(["scr"], aT_n)


Builder.mlp = mlp


def _ada_slots(self):
    return [self.scr[:, 16384 + i * 4096:16384 + (i + 1) * 4096].rearrange("p (k n) -> p k n", k=8) for i in range(2)]


def _ada_start(self, l):
    self.P.adopt(["adaslot0", "adaslot1"], ["scr"])
    self._ada_l = l
    self._ada_issued = 0
    self._ada_used = 0
    self._ada_issue()
    self._ada_issue()


def _ada_issue(self):
    i = self._ada_issued
    if i >= 12:
        return
    l = self._ada_l
    s_ = i % 2
    src = self.d_wada[l].rearrange("(kc p) n -> p kc n", p=128)[:, :, i * 512:(i + 1) * 512]
    self.P.dma("pool", "ada%d" % s_, self._ada_slots()[s_], src, writes=["adaslot%d" % s_])
    self._ada_issued += 1


def _ada_step(self, l, n):
    P = self.P
    assert l == self._ada_l
    for _ in range(n):
        i = self._ada_used
        if i >= 12:
            return
        s_ = i % 2
        w = self._ada_slots()[s_]
        b = self.bank()
        fns = []
        for c in range(4):
            for kc in range(KC):
                fns.append(mm(self.ps[b][:, c:c + 1], w[:, kc, c * 128:(c + 1) * 128], self.scb[:, kc:kc + 1], kc == 0, kc == KC - 1))
        P.group("pe", fns, reads=["adaslot%d" % s_, "scb"], writes=["ps%d" % b])
        P.op("dve", lambda e: e.tensor_tensor(self.mod[:, l, i * 4:(i + 1) * 4], self.ps[b][:, 0:4], self.bada[:, l, i * 4:(i + 1) * 4], ALU.add),
             reads=["ps%d" % b, "bada"], writes=["mod_%d_%d" % (l, i)])
        self._ada_used += 1
        self._ada_issue()


def _ada_finish(self, l):
    self.ada_step(l, 12 - self._ada_used)
    self.P.adopt(["scr"], ["adaslot0", "adaslot1"])


Builder._ada_slots = _ada_slots
Builder.ada_start = _ada_start
Builder._ada_issue = _ada_issue
Builder.ada_step = _ada_step
Builder.ada_finish = _ada_finish


_CACHE = {}


def grid_pos_embed_T():
    rows = NT // 64
    r = np.repeat(np.arange(rows, dtype=np.float32), 64)
    col = np.tile(np.arange(64, dtype=np.float32), rows)
    quarter = D // 4
    freqs = (1.0 / (10000.0 ** (np.arange(quarter, dtype=np.float32) / np.float32(quarter)))).astype(np.float32)
    ar = r[:, None] * freqs
    ac = col[:, None] * freqs
    pe = np.concatenate([np.sin(ar), np.cos(ar), np.sin(ac), np.cos(ac)], axis=-1).astype(np.float32)
    return np.ascontiguousarray(pe.T)


def dft_pieces(Ls):
    l = np.arange(NT)
    seg = l // Ls
    k = ((l % Ls)[:, None] * (l % Ls)[None, :]) % Ls
    ang = 2.0 * np.pi * k.astype(np.float64) / Ls
    same = (seg[:, None] == seg[None, :])
    sc = 1.0 / np.sqrt(Ls * 128.0)
    C = np.where(same, np.cos(ang), 0.0) * sc
    S = np.where(same, -np.sin(ang), 0.0) * sc
    out = np.zeros((16, 128, 8, 512), np.float32)
    for tq in range(4):
        for t, M in enumerate((C, S)):
            blk = M[:, tq * 512:(tq + 1) * 512].reshape(16, 128, 512).transpose(1, 0, 2)
            for half in range(2):
                out[tq * 4 + t * 2 + half] = blk[:, half * 8:(half + 1) * 8, :]
    return out.astype(ml_dtypes.bfloat16)


def state_layout(st):
    out = np.zeros((2, 2, 128, 2, 2, 128), np.float32)
    for h in range(4):
        out[:, :, (h % 2) * 64:(h % 2) * 64 + 64, h // 2, h % 2, :] = st[:, :, h]
    return out.reshape(2, 2, 128, 512)


def state_unlayout(a):
    a = a.reshape(a.shape[:-2] + (128, 2, 2, 128))
    out = np.zeros(a.shape[:-4] + (4, 64, 128), np.float32)
    for h in range(4):
        out[..., h, :, :] = a[..., (h % 2) * 64:(h % 2) * 64 + 64, h // 2, h % 2, :]
    return out


def band_library(Ls):
    out = np.zeros((128, 4, 8, 128), np.float32)
    t = np.arange(NT)
    seg0 = (t // Ls) * Ls
    for g, w in enumerate((2, 4, 8, 16)):
        lo = np.clip(t - w // 2, seg0, seg0 + Ls)
        hi = np.clip(t + w - w // 2, seg0, seg0 + Ls)
        cnt = (hi - lo).astype(np.float64)
        Pm = np.zeros((NT, NT), np.float64)
        for d in range(NT):
            Pm[lo[d]:hi[d], d] = 1.0 / cnt[d]
            Pm[d, d] -= 1.0
        blk = lambda a, b: Pm[a * 128:(a + 1) * 128, b * 128:(b + 1) * 128]
        out[:, g, 0] = blk(0, 0)
        out[:, g, 1] = blk(2, 2)
        out[:, g, 2] = blk(1, 1)
        out[:, g, 3] = blk(15, 15)
        out[:, g, 4] = blk(0, 1)
        out[:, g, 5] = blk(1, 2)
        out[:, g, 6] = blk(1, 0)
        out[:, g, 7] = blk(2, 1)
    return out.reshape(128, 4096).astype(ml_dtypes.bfloat16)


def host_consts():
    cb = np.zeros((128, 768), np.float32)
    tri_f = (np.arange(128)[:, None] <= np.arange(128)[None, :]).astype(np.float32)
    tri_b = (np.arange(128)[:, None] >= np.arange(128)[None, :]).astype(np.float32)
    cb[:, 512:640] = tri_f
    cb[:, 640:768] = tri_b
    cf = np.zeros((128, 260), np.float32)
    cf[:, 0:128] = tri_f
    cf[:, 128:256] = tri_b
    cf[:, 256:260] = 1.0
    cb[:, 0:128] = 1.0 / 1024.0
    cb[:, 128:256] = np.eye(128, dtype=np.float32)
    c = np.arange(128)
    ang = 2.0 * np.pi * ((c[:, None] * c[None, :]) % 128).astype(np.float64) / 128.0
    cb[:, 256:384] = np.cos(ang)
    cb[:, 384:512] = np.sin(ang)
    return {"cbf": cb.astype(ml_dtypes.bfloat16), "cf32": cf}


def make_in_maps(inp, cfg):
    f = lambda a: np.ascontiguousarray(np.asarray(a, dtype=np.float32))
    shared = {
        "b_ada": f(np.asarray(inp["b_ada"]).reshape(DEPTH, 48, 128).transpose(0, 2, 1)),
        "w_ada": f(inp["w_ada"]), "w_mlp1": f(inp["w_mlp1"]), "w_mlp2": f(inp["w_mlp2"]),
        "w_in_odd": f(inp["w_in_odd"]), "w_out_odd": f(inp["w_out_odd"]),
        "w_in_even": f(inp["w_in_even"]), "w_out_even": f(inp["w_out_even"]),
        "w_a2": f(inp["w_a2"]), "b_a2": f(np.asarray(inp["b_a2"]).reshape(2, 512)), "pool_w": f(inp["pool_w"]),
    }
    band_s = band_library(2048)
    band_p = band_library(256)
    psl = np.asarray(inp["pool_s"], np.float32).reshape(2, 4, 128).transpose(2, 0, 1).reshape(128, 8)
    glg = np.asarray(inp["gla_norm_g"], np.float32).T
    sg = np.asarray(inp["state_gla"], np.float32)
    dft_s = dft_pieces(2048)
    dft_p = dft_pieces(256)
    cw = np.asarray(inp["conv_w"], np.float32).reshape(2, 3, 4, 128).transpose(3, 0, 1, 2).reshape(128, 24)
    cbi = np.asarray(inp["conv_b"], np.float32).reshape(2, 4, 128).transpose(2, 0, 1).reshape(128, 8)
    shared.update(host_consts())
    ng = np.asarray(inp["norm_g"], np.float32).reshape(DEPTH, 2, KC, 128).transpose(3, 0, 1, 2).reshape(128, 64)
    fg = np.asarray(inp["final_g"], np.float32).reshape(KC, 128).T
    peT = grid_pos_embed_T()
    peRC = np.zeros((128, 384), np.float32)
    peRC[:, 0:128] = peT[0:512, 0::64].reshape(4, 128, 32).transpose(1, 0, 2).reshape(128, 128)
    peRC[:, 128:384] = peT[512:1024, 0:64].reshape(4, 128, 64).transpose(1, 0, 2).reshape(128, 256)
    zpe = np.zeros_like(peRC)
    xs = np.asarray(inp["x_sample"], np.float32)
    xp = np.asarray(inp["x_prompt"], np.float32)
    maps = []
    dummy = None
    for core in cfg.get("cores", PHYS):
        if core == 6:
            if dummy is None:
                dummy = {k: (v if k in ("cbf", "cf32") else np.zeros_like(v)) for k, v in maps[0].items()}
            maps.append(dummy)
            continue
        m = dict(shared)
        small = np.zeros((128, 192), np.float32)
        small[:, 16:80] = ng
        small[:, 80:88] = fg
        small[:, 96:120] = cw
        small[:, 120:128] = cbi
        small[:, 128:136] = psl
        small[:, 136:138] = glg
        if core < 4:
            xT = xs[core].T
            cond = np.asarray(inp["c"], np.float32)[core]
            small[:, 8] = 1.0
            m["peRC"] = peRC
            m["dftm"] = dft_s
            m["bandm"] = band_s
            m["s0"] = state_layout(sg[core])
        else:
            g = (core - 4) % 2
            xT = xp[g * 8:(g + 1) * 8].reshape(NT, D).T
            cond = np.asarray(inp["c_ctx"], np.float32)
            small[:, 8] = 0.0
            m["peRC"] = zpe
            m["dftm"] = dft_p
            m["bandm"] = band_p
            m["s0"] = np.zeros((2, 2, 128, 512), np.float32)
        small[:, 0:8] = cond.reshape(KC, 128).T
        m["xT"] = np.ascontiguousarray(xT)
        m["small"] = small
        maps.append(m)
    return maps


def get_nc(cfg):
    key = tuple(sorted((k, v) for k, v in cfg.items() if k != "cores"))
    if key not in _CACHE:
        _CACHE[key] = Builder(dict(cfg)).build()
    return _CACHE[key]


def run(inp, cfg, trace=False):
    nc = get_nc(cfg)
    maps = make_in_maps(inp, cfg)
    res = run_bass_kernel_spmd(nc, maps, core_ids=list(range(len(maps))), **({"trace": True} if trace else {}))
    return res


PHYS = (0, 1, 4, 6, 2, 3, 5, 6)


def assemble(res):
    B, S = 16, 256
    y_prompt = np.zeros((B, S, D), np.float32)
    y_sample = np.zeros((4, NT, D), np.float32)
    for b in range(4):
        y_sample[b] = res.results[PHYS.index(b)]["yT"].T
    for g in range(2):
        y_prompt[g * 8:(g + 1) * 8] = res.results[PHYS.index(4 + g)]["yT"].T.reshape(8, S, D)
    return y_prompt, y_sample


CFG = {"depth": 4, "mixer": True, "mlp": True}


def kernel(**inputs):
    res = run(inputs, CFG)
    y_prompt, y_sample = assemble(res)
    st = np.zeros((16, 2, 2, 4, 64, 128), np.float32)
    for g in range(2):
        st[g * 8:(g + 1) * 8] = state_unlayout(res.results[PHYS.index(4 + g)]["st"])
    return (y_prompt, y_sample, st)
```

```python
import numpy as np
import ml_dtypes
import concourse.bass as bass
import concourse.mybir as mybir
from concourse.bass_utils import run_bass_kernel_spmd

F32 = mybir.dt.float32
BF16 = mybir.dt.bfloat16
AF = mybir.ActivationFunctionType
ALU = mybir.AluOpType

D = 1024
KC = 8
NT = 2048
TT = 4
TW = 512
DEPTH = 4
DFF = 4096
EPS = 1e-6
NSLOT = 4
N_CORES = 8


class Res:
    __slots__ = ("name", "w", "r")

    def __init__(self, name):
        self.name = name
        self.w = None
        self.r = {}


class Prog:
    ENG = ("pe", "act", "dve", "pool", "sp")

    def __init__(self, nc, same_engine_sync=True):
        self.nc = nc
        self.eng = {"pe": nc.tensor, "act": nc.scalar, "dve": nc.vector,
                    "pool": nc.gpsimd, "sp": nc.sync}
        self.sems = {}
        self.cnt = {}
        for e in ("pe", "act", "dve", "pool"):
            self.sems[e] = nc.alloc_semaphore("c_" + e)
            self.cnt[e] = 0
        self.waited = {e: {} for e in self.ENG}
        self.same_engine_sync = same_engine_sync
        self.nwaits = 0
        self.ninst = {e: 0 for e in self.ENG}
        self.res = {}

    def R(self, name):
        r = self.res.get(name)
        if r is None:
            r = Res(name)
            self.res[name] = r
        return r

    def dma_sem(self, name):
        k = "d_" + name
        if k not in self.sems:
            self.sems[k] = self.nc.alloc_semaphore(k)
            self.cnt[k] = 0
        return k

    def _wait(self, e, tok):
        if tok is None:
            return
        k, v = tok
        if k == e and (e == "pe" or not self.same_engine_sync):
            return
        if self.waited[e].get(k, 0) >= v:
            return
        self.waited[e][k] = v
        self.eng[e].wait_ge(self.sems[k], v)
        self.nwaits += 1

    def _deps(self, e, reads, writes):
        for r in reads:
            self._wait(e, self.R(r).w)
        for r in writes:
            r = self.R(r)
            self._wait(e, r.w)
            for k, v in list(r.r.items()):
                self._wait(e, (k, v))

    def _commit(self, tok, reads, writes):
        k, v = tok
        for r in reads:
            r = self.R(r)
            if r.r.get(k, 0) < v:
                r.r[k] = v
        for r in writes:
            r = self.R(r)
            r.w = tok
            r.r = {}

    def op(self, e, fn, reads=(), writes=()):
        self._deps(e, reads, writes)
        ins = fn(self.eng[e])
        self.cnt[e] += 1
        ins.then_inc(self.sems[e], 1)
        self.ninst[e] += 1
        tok = (e, self.cnt[e])
        self._commit(tok, reads, writes)
        return tok

    def group(self, e, fns, reads=(), writes=()):
        self._deps(e, reads, writes)
        ins = None
        for fn in fns:
            ins = fn(self.eng[e])
            self.ninst[e] += 1
        self.cnt[e] += 1
        ins.then_inc(self.sems[e], 1)
        tok = (e, self.cnt[e])
        self._commit(tok, reads, writes)
        return tok

    def dma(self, q, semname, out, in_, reads=(), writes=(), **kw):
        k = self.dma_sem(semname)
        self._deps(q, reads, writes)
        if self.cnt[k] > 0:
            self._wait(q, (k, self.cnt[k]))
        ins = self.eng[q].dma_start(out=out, in_=in_, **kw)
        self.cnt[k] += 16
        ins.then_inc(self.sems[k], 16)
        self.ninst[q] += 1
        tok = (k, self.cnt[k])
        self._commit(tok, reads, writes)
        return tok

    def dma_batch(self, q, semname, items, **kw):
        k = self.dma_sem(semname)
        for (out, in_, reads, writes) in items:
            self._deps(q, reads, writes)
        for (out, in_, reads, writes) in items:
            if self.cnt[k] > 0:
                self._wait(q, (k, self.cnt[k]))
            ins = self.eng[q].dma_start(out=out, in_=in_, **kw)
            self.cnt[k] += 16
            ins.then_inc(self.sems[k], 16)
            self.ninst[q] += 1
        tok = (k, self.cnt[k])
        for (out, in_, reads, writes) in items:
            self._commit(tok, reads, writes)
        return tok

    def adopt(self, new_names, old_names):
        toks = {}
        for o in old_names:
            o = self.R(o)
            if o.w is not None:
                k, v = o.w
                toks[k] = max(toks.get(k, 0), v)
            for k, v in o.r.items():
                toks[k] = max(toks.get(k, 0), v)
        for n in new_names:
            n = self.R(n)
            for k, v in toks.items():
                if n.r.get(k, 0) < v:
                    n.r[k] = v

    def finish(self, e, names):
        for n in names:
            n = self.R(n)
            self._wait(e, n.w)
            for k, v in list(n.r.items()):
                self._wait(e, (k, v))


def mm(out, lhsT, rhs, start, stop):
    return lambda e: e.matmul(out, lhsT, rhs, start=start, stop=stop)


class Builder:
    def __init__(self, cfg):
        self.cfg = cfg
        nc = bass.Bass("TRN2", target_bir_lowering=False)
        self.nc = nc
        self.P = Prog(nc, same_engine_sync=cfg.get("ses", True))
        self.declare_dram()
        self.alloc()
        self.psum_i = 0
        self.wq = []
        self.wq_issued = 0
        self.wq_used = 0

    def declare_dram(self):
        nc = self.nc

        def din(name, shape, dt=F32):
            return nc.dram_tensor(name, list(shape), dt, kind="ExternalInput").ap()

        def dout(name, shape, dt=F32):
            return nc.dram_tensor(name, list(shape), dt, kind="ExternalOutput").ap()

        self.d_xT = din("xT", [D, NT])
        self.d_peRC = din("peRC", [128, 4 * 32 + 4 * 64])
        self.d_small = din("small", [128, 192])
        self.d_bada = din("b_ada", [DEPTH, 128, 48])
        self.d_wada = din("w_ada", [DEPTH, D, 6 * D])
        self.d_w1 = din("w_mlp1", [DEPTH, D, DFF])
        self.d_w2 = din("w_mlp2", [DEPTH, DFF, D])
        self.d_cb = din("cbf", [128, 768], BF16)
        self.d_cf = din("cf32", [128, 260])
        self.d_wio = din("w_in_odd", [2, D, 2048])
        self.d_woo = din("w_out_odd", [2, D, D])
        self.d_dft = din("dftm", [16, 128, 8, 512], BF16)
        self.d_wie = din("w_in_even", [2, D, 2080])
        self.d_woe = din("w_out_even", [2, D, D])
        self.d_wa2 = din("w_a2", [2, 2, 16, 256])
        self.d_ba2 = din("b_a2", [2, 512])
        self.d_poolw = din("pool_w", [2, 4, 128, 128])
        self.d_band = din("bandm", [128, 4096], BF16)
        self.d_s0 = din("s0", [2, 2, 128, 512])
        self.d_st = dout("st", [8, 2, 2, 128, 512])
        self.d_yT = dout("yT", [D, NT])

    def alloc(self):
        nc = self.nc
        A = lambda name, shape, dt: nc.alloc_sbuf_tensor("s_" + name, list(shape), dt).ap()
        self.x = A("x", [128, KC, NT], F32)
        self.hT = A("hT", [128, KC, NT], BF16)
        self.scr = A("scr", [128, 24576], BF16)
        self.wring = [A("wr%d" % i, [128, 4096], BF16) for i in range(NSLOT)]
        self.small = A("small", [128, 192], F32)
        self.cb = A("cb", [128, 768], BF16)
        self.cf = A("cf", [128, 260], F32)
        self.bada = A("bada", [128, DEPTH, 48], F32)
        self.mod = A("mod", [128, DEPTH, 48], F32)
        self.modA = A("modA", [128, DEPTH, 2, KC], F32)
        self.scb = A("scb", [128, KC], BF16)
        self.ss = A("ss", [128, 13312], BF16)
        self.sq = [self.carve(i * 2048, [128, 2, TW], BF16) for i in range(2)]
        self.rstd = [self.carve(4096 + i * 2048, [128, TW], F32) for i in range(2)]
        self.tmp = [self.carve(8192 + i * 2048, [128, TW], F32) for i in range(3)]
        self.rl = [self.carve(14336 + i * 1024, [128, TW], BF16) for i in range(3)]
        self.ss_names = ["sq0", "sq1", "rstd0", "rstd1", "tmp0", "tmp1", "tmp2", "rl0", "rl1", "rl2"]
        self.ps = [nc.alloc_psum_tensor("ps%d" % i, [128, TW], F32).ap() for i in range(8)]
        self.i_sq = self.i_rstd = self.i_tmp = self.i_rl = 0

    def carve(self, off, shape, dt, base=None, nparts=128):
        base = self.ss if base is None else base
        n = int(np.prod(shape[1:]))
        if dt == F32:
            v = base[0:nparts, off // 2: off // 2 + 2 * n].bitcast(F32)
        else:
            v = base[0:nparts, off // 2: off // 2 + n]
        if len(shape) == 2:
            return v
        names = "abcdef"[:len(shape) - 1]
        kw = {names[i]: shape[1 + i] for i in range(len(shape) - 2)}
        return v.rearrange("p (%s) -> p %s" % (" ".join(names), " ".join(names)), **kw)

    def bank(self):
        b = self.psum_i % 8
        self.psum_i += 1
        return b

    def wq_add(self, name, src_ap, view, hold=0):
        self.wq.append((name, src_ap, view, hold))

    def wq_pump(self, i):
        P = self.P
        while self.wq_issued < len(self.wq):
            k = self.wq_issued
            ev = k - NSLOT
            if ev >= 0 and ev + self.wq[ev][3] >= i:
                break
            name, src, view, hold = self.wq[k]
            s = k % NSLOT
            dst = self.wview(s, view)
            if isinstance(src, list):
                P.dma_batch("pool", "w%d" % s, [(sel(dst), sap, [], ["wslot%d" % s]) for sel, sap in src])
            else:
                P.dma("pool", "w%d" % s, dst, src, writes=["wslot%d" % s])
            self.wq_issued += 1

    def wq_get(self, name, ahead=None):
        i = self.wq_used
        n, src, view, hold = self.wq[i]
        assert n == name, (n, name)
        self.wq_pump(i)
        assert self.wq_issued > i
        self.wq_used += 1
        s = i % NSLOT
        return self.wview(s, view), "wslot%d" % s

    def wview(self, s, view):
        n = int(np.prod(view))
        names = "abcd"[:len(view)]
        kw = {names[i]: view[i] for i in range(len(view) - 1)}
        return self.wring[s][:, 0:n].rearrange("p (%s) -> p %s" % (" ".join(names), " ".join(names)), **kw)

    def plan_weights(self):
        cfg = self.cfg
        mlp_on = cfg.get("mlp", True)
        for l in range(cfg["depth"]):
            self.plan_mixer(l)
            for hb in range(4 if mlp_on else 0):
                for j in range(2):
                    c0 = hb * 1024 + j * 512
                    self.wq_add("w1_%d_%d_%d" % (l, hb, j),
                                self.d_w1[l].rearrange("(kc p) n -> p kc n", p=128)[:, :, c0:c0 + 512], (8, 512))
                for j in range(2):
                    r0 = hb * 1024 + j * 512
                    self.wq_add("w2_%d_%d_%d" % (l, hb, j),
                                self.d_w2[l, r0:r0 + 512, :].rearrange("(kc p) n -> p kc n", p=128), (4, 1024), hold=1 - j)

    def plan_mixer(self, l):
        if not self.cfg.get("mixer", True):
            return
        jl = l // 2
        if l % 2 == 1:
            if not self.cfg.get("odd", True):
                return
            wv = self.d_wio[jl].rearrange("(kc p) n -> p kc n", p=128)
            self.wq_add("wio_f_%d" % l, wv[:, :, 0:512], (8, 512))
            for i in range(16):
                self.wq_add("dft_%d_%d" % (l, i), self.d_dft[i], (8, 512))
            wo = self.d_woo[jl].rearrange("(kc p) n -> p kc n", p=128)
            self.wq_add("woo_A_%d" % l, wo[:, 0:4, :], (4, 1024))
            for ci in range(4):
                parts = []
                for blk in range(3):
                    c0 = 512 + blk * 512 + ci * 128
                    parts.append(((lambda blk: (lambda v: v[:, :, blk, :]))(blk), wv[:, :, c0:c0 + 128]))
                self.wq_add("wio_c_%d_%d" % (l, ci), parts, (8, 3, 128))
            self.wq_add("woo_B_%d" % l, wo[:, 4:8, :], (4, 1024))
        else:
            if not self.cfg.get("even", True):
                return
            self.plan_even(l)

    def plan_even(self, l):
        jl = l // 2
        wv = self.d_wie[jl].rearrange("(kc p) n -> p kc n", p=128)
        wo = self.d_woe[jl].rearrange("(kc p) n -> p kc n", p=128)
        self.wq_add("wie_u_%d" % l, wv[:, :, 1568:2080], (8, 512))
        self.wq_add("band_%d" % l, self.d_band.rearrange("p (g k t) -> p g k t", g=4, k=8), (4, 8, 128), hold=1)
        self.wq_add("poolw_%d" % l, self.d_poolw[jl].rearrange("g c d -> c g d"), (4, 128))
        self.wq_add("woe_A_%d" % l, wo[:, 4:8, :], (4, 1024))
        if self.cfg.get("estop", 9) <= 1:
            return
        self.wq_add("wie_glr_%d" % l, wv[:, :, 1536:1568], (8, 32))
        self.wq_add("wie_qk_%d" % l, wv[:, :, 0:512], (8, 512))
        self.wq_add("wie_v_%d" % l, wv[:, :, 512:1024], (8, 512))
        self.wq_add("wie_g_%d" % l, wv[:, :, 1024:1536], (8, 512))
        if self.cfg.get("estop", 9) <= 4:
            return
        self.wq_add("woe_B_%d" % l, wo[:, 0:4, :], (4, 1024))

    def prologue(self, after_tt=None):
        P, nc = self.P, self.nc
        P.dma("sp", "c0", self.small, self.d_small, writes=["small"])
        P.dma("sp", "c1", self.cb, self.d_cb, writes=["cb"])
        P.dma("sp", "c3", self.cf, self.d_cf, writes=["cf"])
        P.dma("sp", "c2", self.bada, self.d_bada.rearrange("l p c -> p l c"), writes=["bada"])
        self.cond = self.small[:, 0:8]
        self.keep = self.small[:, 8:9]
        self.ng = self.small[:, 16:80].rearrange("p (l s k) -> p l s k", l=DEPTH, s=2)
        self.fg = self.small[:, 80:88]
        self.ones_bf = self.cb[:, 0:128]
        self.ident_bf = self.cb[:, 128:256]
        P.op("act", lambda e: e.activation(self.scb, self.cond, AF.Silu), reads=["small"], writes=["scb"])
        self.ada_start(0)
        peRC = self.scr[:, 0:768].bitcast(F32)
        P.dma("sp", "pin0", peRC, self.d_peRC, writes=["scrpe_0"])
        peR = peRC[:, 0:128].rearrange("p (k r) -> p k r", k=4)
        peC = peRC[:, 128:384].rearrange("p (k c) -> p k c", k=4)
        xv = self.d_xT.rearrange("(kc p) t -> p kc t", p=128)
        for tt in range(TT):
            sl = slice(tt * TW, (tt + 1) * TW)
            xres = ["x_%d_%d" % (kc, tt) for kc in range(KC)]
            P.dma("sp", "xin%d" % tt, self.x[:, :, sl], xv[:, :, sl], writes=xres)
            xr_ = self.x[:, 0:4, sl].rearrange("p k (r c) -> p k r c", c=64)
            xc_ = self.x[:, 4:8, sl].rearrange("p k (r c) -> p k r c", c=64)
            P.op("dve", lambda e: e.tensor_tensor(xr_, xr_, peR[:, :, 8 * tt:8 * tt + 8].unsqueeze(3).broadcast_to([128, 4, 8, 64]), ALU.add),
                 reads=["scrpe_0"], writes=xres[0:4])
            P.op("dve", lambda e: e.tensor_tensor(xc_, xc_, peC.unsqueeze(2).broadcast_to([128, 4, 8, 64]), ALU.add),
                 reads=["scrpe_0"], writes=xres[4:8])
            if tt == 0:
                self.ada_step(0, 4)
            if after_tt is not None:
                after_tt(tt)
        P.adopt(["scr"], ["scrpe_0"])

    def mod_part(self, l, part):
        return self.mod[:, l, part * 8:(part + 1) * 8]

    def mod_res(self, l, part):
        return ["mod_%d_%d" % (l, part * 2), "mod_%d_%d" % (l, part * 2 + 1)]

    def prep_modA(self, l, s):
        P = self.P
        sc = self.mod_part(l, 1 + 3 * s)
        P.op("dve", lambda e: e.scalar_tensor_tensor(self.modA[:, l, s, :], sc, 1.0, self.ng[:, l, s, :], ALU.add, ALU.mult),
             reads=self.mod_res(l, 1 + 3 * s) + ["small"], writes=["modA_%d_%d" % (l, s)])

    def norm_begin(self, l, s, final=False):
        if not final:
            self.prep_modA(l, s)
            return dict(final=False, A=self.modA[:, l, s, :], B=self.mod_part(l, 3 * s),
                        dep=["modA_%d_%d" % (l, s)] + self.mod_res(l, 3 * s))
        return dict(final=True, A=self.fg, B=None, dep=["small"])

    def norm_tt(self, ctx, tt):
        P = self.P
        A, B, dep, final = ctx["A"], ctx["B"], ctx["dep"], ctx["final"]
        sl = slice(tt * TW, (tt + 1) * TW)
        b = self.bank()
        for k2 in range(KC // 2):
            sq = self.sq[self.i_sq % 2]
            sqn = "sq%d" % (self.i_sq % 2)
            self.i_sq += 1
            P.op("act", lambda e: e.activation(sq, self.x[:, 2 * k2:2 * k2 + 2, sl], AF.Square),
                 reads=["x_%d_%d" % (2 * k2, tt), "x_%d_%d" % (2 * k2 + 1, tt)], writes=[sqn])
            P.group("pe", [mm(self.ps[b], self.ones_bf, sq[:, j, :], k2 == 0 and j == 0, k2 == KC // 2 - 1 and j == 1) for j in range(2)],
                    reads=[sqn, "cb"], writes=["ps%d" % b])
        rs = self.rstd[self.i_rstd % 2]
        rsn = "rstd%d" % (self.i_rstd % 2)
        self.i_rstd += 1
        P.op("act", lambda e: e.activation(rs, self.ps[b], AF.Ln, bias=EPS), reads=["ps%d" % b], writes=[rsn])
        P.op("act", lambda e: e.activation(rs, rs, AF.Exp, scale=-0.5), reads=[rsn], writes=[rsn])
        for kc in range(KC):
            xr = "x_%d_%d" % (kc, tt)
            if final:
                P.op("dve", lambda e: e.scalar_tensor_tensor(self.x[:, kc, sl], self.x[:, kc, sl], A[:, kc:kc + 1], rs, ALU.mult, ALU.mult),
                     reads=[xr, rsn] + dep, writes=[xr])
                P.dma("sp", "yout%d" % kc, self.d_yT.rearrange("(kc p) t -> p kc t", p=128)[:, kc, sl], self.x[:, kc, sl], reads=[xr], writes=["yT_%d_%d" % (kc, tt)])
                continue
            t = self.tmp[self.i_tmp % 3]
            tn = "tmp%d" % (self.i_tmp % 3)
            self.i_tmp += 1
            P.op("dve", lambda e: e.scalar_tensor_tensor(t, self.x[:, kc, sl], A[:, kc:kc + 1], rs, ALU.mult, ALU.mult),
                 reads=[xr, rsn] + dep, writes=[tn])
            if True:
                P.op("act", lambda e: e.activation(self.hT[:, kc, sl], t, AF.Identity, bias=B[:, kc:kc + 1], scale=1.0),
                     reads=[tn] + dep, writes=["hT_%d_%d" % (kc, tt)])

    def norm(self, l, s, final=False):
        ctx = self.norm_begin(l, s, final)
        for tt in range(TT):
            self.norm_tt(ctx, tt)

    def mixer(self, l, after_tt=None):
        if l % 2 == 1 and self.cfg.get("odd", True):
            self.mixer_odd(l, after_tt)
        elif l % 2 == 0 and self.cfg.get("even", True):
            self.mixer_even(l, after_tt)
        elif after_tt is not None:
            for tt in range(TT):
                after_tt(tt)

    def cp(self, eng, out, in_):
        return (lambda e: e.copy(out, in_)) if eng == "act" else (lambda e: e.tensor_copy(out, in_))

    def mixer_even(self, l, after_tt=None):
        P = self.P
        nc = self.nc
        jl = l // 2
        scr, ps, hT, cb, cf, small = self.scr, self.ps, self.hT, self.cb, self.cf, self.small
        ones_bf, ident = self.ones_bf, self.ident_bf
        hT_all = ["hT_%d_%d" % (kc, tt) for kc in range(KC) for tt in range(TT)]
        ssn = []

        def C(name, off, shape, dt, nparts=128):
            ssn.append(name)
            return self.carve(off, shape, dt, nparts=nparts)
        glrT = C("glrT", 0, [64, NT], BF16, nparts=64)
        wa2b = C("wa2b", 4096, [64, 512], BF16, nparts=64)
        el = [C("el%d" % i, 5120 + 1024 * i, [128, 256], F32) for i in range(2)]
        Ep = C("Ep", 7168, [128, 256], F32)
        Em = C("Em", 8192, [128, 256], F32)
        qt = [C("qt%d" % i, 9216 + 512 * i, [128, 256], BF16) for i in range(2)]
        kt = [C("kt%d" % i, 10240 + 512 * i, [128, 256], BF16) for i in range(2)]
        qkT = [C("qkT%d" % i, 11264 + 1024 * i, [128, 4, 128], BF16) for i in range(2)]
        attm = [C("attm%d_0" % i, 13312 + 1024 * i, [128, 4, 128], BF16) for i in range(2)]
        ssn.extend(["attm0_1", "attm1_1", "otmp1"])
        S = C("S", 15360, [128, 512], F32)
        Sbf = C("Sbf", 17408, [128, 512], BF16)
        tmpS = C("tmpS", 18432, [128, 512], F32)
        otmp = C("otmp0", 20480, [128, 512], F32)
        sqo = C("sqo", 22528, [128, 512], BF16)
        rso = C("rso", 23552, [128, 512], F32)
        edec = [C("edec%d" % i, 25600 + 16 * i, [128, 2], F32) for i in range(2)]
        ptmp = [self.carve(20480 + 1024 * i, [128, 512], BF16) for i in range(2)]
        P.adopt(ssn + ["ptmp0", "ptmp1"], self.ss_names)
        pool_s = lambda g: small[:, 128 + jl * 4 + g: 128 + jl * 4 + g + 1]
        gla_g = small[:, 136 + jl: 137 + jl]

        utok = scr[:, 0:8192].rearrange("p (j c) -> p j c", j=16)
        ypT = scr[:, 8192:16384].rearrange("p (g t) -> p g t", g=4)
        ut_n = ["ut_%d" % j for j in range(16)]
        yp_n = ["yp_%d_%d" % (g, tt) for g in range(4) for tt in range(TT)]
        P.adopt(ut_n + yp_n, ["scr"])
        wu, wures = self.wq_get("wie_u_%d" % l)
        for j in range(16):
            b = self.bank()
            P.group("pe", [mm(ps[b], hT[:, kc, j * 128:(j + 1) * 128], wu[:, kc, :], kc == 0, kc == KC - 1) for kc in range(KC)],
                    reads=[wures] + ["hT_%d_%d" % (kc, j // 4) for kc in range(KC)], writes=["ps%d" % b])
            eng = "act" if j % 2 else "dve"
            P.op(eng, self.cp(eng, utok[:, j, :], ps[b]), reads=["ps%d" % b], writes=["ut_%d" % j])
        band, bandres = self.wq_get("band_%d" % l)
        pw, pwres = self.wq_get("poolw_%d" % l)
        KD = {"D_start": 0, "D_me": 1, "D_mo": 2, "D_end": 3, "L_same": 4, "L_cross": 5, "U_same": 6, "U_cross": 7}
        items = [(tt, g) for tt in range(TT) for g in range(4)]
        stash = {}

        def pool_stage1(i):
            tt, g = items[i]
            b = self.bank()
            fns = []
            rd = set()
            for jj in range(4):
                j = tt * 4 + jj
                o = ps[b][:, jj * 128:(jj + 1) * 128]
                srcs = []
                if j > 0:
                    srcs.append((j - 1, KD["L_cross"] if j % 2 == 0 else KD["L_same"]))
                srcs.append((j, KD["D_start"] if j == 0 else KD["D_end"] if j == 15 else KD["D_me"] if j % 2 == 0 else KD["D_mo"]))
                if j < 15:
                    srcs.append((j + 1, KD["U_cross"] if j % 2 == 1 else KD["U_same"]))
                for n_, (js, kind) in enumerate(srcs):
                    fns.append(mm(o, utok[:, js, g * 128:(g + 1) * 128], band[:, g, kind, :], n_ == 0, n_ == len(srcs) - 1))
                    rd.add("ut_%d" % js)
            P.group("pe", fns, reads=[bandres] + sorted(rd), writes=["ps%d" % b])
            pt = ptmp[i % 2]
            ptn = "ptmp%d" % (i % 2)
            P.op("act", self.cp("act", pt, ps[b]), reads=["ps%d" % b], writes=[ptn])
            stash[i] = (pt, ptn)

        def pool_stage2(i):
            tt, g = items[i]
            pt, ptn = stash.pop(i)
            b2 = self.bank()
            P.group("pe", [mm(ps[b2], pw[:, g, :], pt, True, True)], reads=[pwres, ptn], writes=["ps%d" % b2])
            P.op("dve", lambda e: e.tensor_scalar(ypT[:, g, tt * TW:(tt + 1) * TW], ps[b2], pool_s(g), None, ALU.mult),
                 reads=["ps%d" % b2, "small"], writes=["yp_%d_%d" % (g, tt)])
            if l == 0 and getattr(self, "_ada0_deferred", False) and tt < 2:
                self.ada_step(0, 1)

        pool_stage1(0)
        for i in range(len(items)):
            if i + 1 < len(items):
                pool_stage1(i + 1)
            pool_stage2(i)
        if l == 0 and getattr(self, "_ada0_deferred", False):
            self.ada_finish(0)
        if self.cfg.get("nopool"):
            self.wq_get("woe_A_%d" % l)
        else:
            self.out_proj(l, "woe_A_%d" % l, lambda kc, tt: ypT[:, kc, tt * TW:(tt + 1) * TW], lambda kc, tt: ["yp_%d_%d" % (kc, tt)])

        estop = self.cfg.get("estop", 9)
        if estop <= 1:
            P.adopt(["scr"], ut_n + yp_n)
            P.adopt(self.ss_names, ssn + ["ptmp0", "ptmp1"])
            return
        P.op("pool", lambda e: e.memset(glrT[32:64, :], 1.0), writes=["glrT"])
        P.op("pool", lambda e: e.memset(wa2b, 0.0), writes=["wa2b"])
        P.dma_batch("pool", "wa2", [
            (wa2b[0:16, 0:256], self.d_wa2[jl, 0], [], ["wa2b"]),
            (wa2b[16:32, 256:512], self.d_wa2[jl, 1], [], ["wa2b"]),
            (wa2b[32:33, :], self.d_ba2[jl:jl + 1, :], [], ["wa2b"])])
        wg_, wgres = self.wq_get("wie_glr_%d" % l)
        for tt in range(TT):
            sl = slice(tt * TW, (tt + 1) * TW)
            b = self.bank()
            P.group("pe", [mm(ps[b][0:32, :], wg_[:, kc, :], hT[:, kc, sl], kc == 0, kc == KC - 1) for kc in range(KC)],
                    reads=[wgres] + ["hT_%d_%d" % (kc, tt) for kc in range(KC)], writes=["ps%d" % b])
            P.op("act", self.cp("act", glrT[0:32, sl], ps[b][0:32, :]), reads=["ps%d" % b], writes=["glrT"])
        qkraw = scr[:, 0:8192].rearrange("p (j c) -> p j c", j=16)
        vtok = scr[:, 8192:16384].rearrange("p (j c) -> p j c", j=16)
        sgT = scr[:, 16384:24576].rearrange("p (h t) -> p h t", h=4)
        qk_n = ["qk_%d" % j for j in range(16)]
        v_n = ["v_%d" % j for j in range(16)]
        sg_n = ["sg_%d_%d" % (h, tt) for h in range(4) for tt in range(TT)]
        P.adopt(qk_n, ut_n)
        P.adopt(v_n, yp_n)
        P.adopt(sg_n, ["scr"])
        for nm, dst, rn in (("wie_qk_%d" % l, qkraw, "qk_%d"), ("wie_v_%d" % l, vtok, "v_%d")):
            w_, wres_ = self.wq_get(nm)
            for j in range(16):
                b = self.bank()
                P.group("pe", [mm(ps[b], hT[:, kc, j * 128:(j + 1) * 128], w_[:, kc, :], kc == 0, kc == KC - 1) for kc in range(KC)],
                        reads=[wres_] + ["hT_%d_%d" % (kc, j // 4) for kc in range(KC)], writes=["ps%d" % b])
                eng = "act" if j % 2 else "dve"
                P.op(eng, self.cp(eng, dst[:, j, :], ps[b]), reads=["ps%d" % b], writes=[rn % j])
        w_, wres_ = self.wq_get("wie_g_%d" % l)

        def g_group(n):
            h, tt = n // 4, n % 4
            sl = slice(tt * TW, (tt + 1) * TW)
            b = self.bank()
            P.group("pe", [mm(ps[b], w_[:, kc, h * 128:(h + 1) * 128], hT[:, kc, sl], kc == 0, kc == KC - 1) for kc in range(KC)],
                    reads=[wres_] + ["hT_%d_%d" % (kc, tt) for kc in range(KC)], writes=["ps%d" % b])
            P.op("act", lambda e: e.activation(sgT[:, h, sl], ps[b], AF.Silu), reads=["ps%d" % b], writes=["sg_%d_%d" % (h, tt)])
        for n in range(16):
            g_group(n)
        if estop <= 2:
            P.adopt(["scr"], qk_n + v_n + sg_n)
            P.adopt(self.ss_names, ssn + ["ptmp0", "ptmp1"])
            return
        if self.cfg.get("gla2", True):
            self.gla_v2(l, jl, glrT, wa2b, qkraw, vtok, sgT, ssn, hT_all, gla_g)
            P.adopt(self.ss_names, ssn + ["ptmp0", "ptmp1"])
            if self.cfg.get("nogla"):
                self.wq_get("woe_B_%d" % l)
                if after_tt is not None:
                    for tt in range(TT):
                        after_tt(tt)
            else:
                self.out_proj(l, "woe_B_%d" % l,
                              lambda kc, tt: qkraw[:, 4 * tt:4 * tt + 4, kc * 128:(kc + 1) * 128],
                              lambda kc, tt: ["qk_%d" % (4 * tt + i) for i in range(4)], after_tt=after_tt)
            P.adopt(["scr"], qk_n + v_n + sg_n)
            return
        ofT = hT[:, :, :].rearrange("p k t -> p (k t)").bitcast(F32).rearrange("p (h t) -> p h t", h=4)
        of_n = ["of_%d_%d" % (j, par) for j in range(16) for par in range(2)]
        P.adopt(of_n, hT_all)
        tri32 = [cf[:, 0:128], cf[:, 128:256]]
        onec = cf[:, 256:257]
        maskb = [cb[:, 512:640], cb[:, 640:768]]
        psb = lambda b: ps[b].bitcast(BF16)

        for z in range(2):
            order = list(range(16)) if z == 0 else list(range(15, -1, -1))
            P.dma("sp", "s0in", S, self.d_s0[jl, z], writes=["S"])
            P.op("act", self.cp("act", Sbf, S), reads=["S"], writes=["Sbf"])

            def stageA(n, j):
                r = n % 2
                b = self.bank()
                P.group("pe", [mm(ps[b][:, 0:256], glrT[0:33, j * 128:(j + 1) * 128], wa2b[0:33, z * 256:(z + 1) * 256], True, True)],
                        reads=["glrT", "wa2b"], writes=["ps%d" % b])
                e_ = el[r]
                en = "el%d" % r
                P.op("act", lambda e: e.activation(e_, ps[b][:, 0:256], AF.Exp, scale=-1.0), reads=["ps%d" % b], writes=[en])
                P.op("act", lambda e: e.activation(e_, e_, AF.Ln, bias=1.0), reads=[en], writes=[en])
                b2 = self.bank()
                P.group("pe", [mm(ps[b2][:, 0:256], tri32[z], e_, True, True),
                               mm(ps[b2][:, 256:257], e_[:, 0:128], onec, True, True),
                               mm(ps[b2][:, 257:258], e_[:, 128:256], onec, True, True)],
                        reads=[en, "cf"], writes=["ps%d" % b2])
                P.op("act", lambda e: e.activation(Ep, ps[b2][:, 0:256], AF.Exp, scale=-1.0 / 16.0), reads=["ps%d" % b2], writes=["Ep"])
                P.op("act", lambda e: e.activation(Em, ps[b2][:, 0:256], AF.Exp, scale=1.0 / 16.0), reads=["ps%d" % b2], writes=["Em"])
                P.op("act", lambda e: e.activation(edec[r], ps[b2][:, 256:258], AF.Exp, scale=-1.0 / 16.0), reads=["ps%d" % b2], writes=["edec%d" % r])
                P.op("dve", lambda e: e.scalar_tensor_tensor(qt[r], qkraw[:, j, 0:256], 0.125, Ep, ALU.mult, ALU.mult), reads=["qk_%d" % j, "Ep"], writes=["qt%d" % r])
                P.op("dve", lambda e: e.tensor_tensor(kt[r], qkraw[:, j, 256:512], Em, ALU.mult), reads=["qk_%d" % j, "Em"], writes=["kt%d" % r])
                b3 = self.bank()
                pb = psb(b3)
                fns = []
                for i4 in range(4):
                    src = (qt[r] if i4 < 2 else kt[r])[:, (i4 % 2) * 128:(i4 % 2 + 1) * 128]
                    fns.append((lambda i4, src: (lambda e: e.transpose(pb[:, i4 * 128:(i4 + 1) * 128], src, ident)))(i4, src))
                P.group("pe", fns, reads=["qt%d" % r, "kt%d" % r, "cb"], writes=["ps%d" % b3])
                P.op("act", self.cp("act", qkT[r], pb[:, 0:512].rearrange("p (a t) -> p a t", a=4)), reads=["ps%d" % b3], writes=["qkT%d" % r])

            def stageB(n, j):
                r = n % 2
                T_ = qkT[r]
                bA, bB = self.bank(), self.bank()
                hb_ = lambda h: (bA if h % 2 == 0 else bB)
                fns = []
                for h in range(4):
                    lo = (h % 2) * 64
                    fns.append(mm(ps[hb_(h)][:, (h // 2) * 128:(h // 2 + 1) * 128], T_[lo:lo + 64, 2 + h // 2, :], T_[lo:lo + 64, h // 2, :], True, True))
                P.group("pe", fns, reads=["qkT%d" % r], writes=["ps%d" % bA, "ps%d" % bB])
                am = attm[r]
                for par, bq in ((0, bA), (1, bB)):
                    P.op("dve", (lambda par, bq: (lambda e: e.tensor_tensor(am[:, par::2, :], ps[bq][:, 0:256].rearrange("p (h t) -> p h t", h=2),
                                                                            maskb[z].unsqueeze(1).broadcast_to([128, 2, 128]), ALU.mult)))(par, bq),
                         reads=["ps%d" % bq, "cb"], writes=["attm%d_%d" % (r, par)])
                oA, oB = self.bank(), self.bank()
                ob_ = lambda h: (oA if h % 2 == 0 else oB)
                fns = []
                for h in range(4):
                    lo = (h % 2) * 64
                    o = ps[ob_(h)][:, (h // 2) * 128:(h // 2 + 1) * 128]
                    fns.append(mm(o, vtok[:, j, h * 128:(h + 1) * 128], am[:, h, :], True, False))
                    fns.append(mm(o, Sbf[lo:lo + 64, (h // 2) * 256 + (h % 2) * 128:(h // 2) * 256 + (h % 2) * 128 + 128], T_[lo:lo + 64, h // 2, :], False, True))
                P.group("pe", fns, reads=["v_%d" % j, "attm%d_0" % r, "attm%d_1" % r, "Sbf", "qkT%d" % r], writes=["ps%d" % oA, "ps%d" % oB])
                b3 = self.bank()
                P.group("pe", [mm(ps[b3][:, hp * 256:(hp + 1) * 256], kt[r][:, hp * 128:(hp + 1) * 128], vtok[:, j, hp * 256:(hp + 1) * 256], True, True) for hp in range(2)],
                        reads=["kt%d" % r, "v_%d" % j], writes=["ps%d" % b3])
                P.op("dve", lambda e: e.tensor_tensor(tmpS, ps[b3], S, ALU.add), reads=["ps%d" % b3, "S"], writes=["tmpS"])
                seg_end = (j % 2 == 1) if z == 0 else (j % 2 == 0)
                edb = edec[r].unsqueeze(2).broadcast_to([128, 2, 256])
                t3 = tmpS.rearrange("p (a c) -> p a c", a=2)
                if seg_end:
                    P.op("dve", lambda e: e.tensor_tensor(t3, t3, edb, ALU.mult), reads=["tmpS", "edec%d" % r], writes=["tmpS"])
                    P.dma("sp", "stout", self.d_st[j // 2, jl, z], tmpS, reads=["tmpS"], writes=["st_%d_%d_%d" % (j // 2, jl, z)])
                    P.op("dve", lambda e: e.tensor_scalar(S, tmpS, self.keep, None, ALU.mult), reads=["tmpS", "small"], writes=["S"])
                else:
                    P.op("dve", lambda e: e.tensor_tensor(S.rearrange("p (a c) -> p a c", a=2), t3, edb, ALU.mult),
                         reads=["tmpS", "edec%d" % r], writes=["S"])
                P.op("act", self.cp("act", Sbf, S), reads=["S"], writes=["Sbf"])
                if z == 0:
                    for par, bq in ((0, oA), (1, oB)):
                        P.op("act", self.cp("act", ofT[:, par::2, j * 128:(j + 1) * 128], ps[bq][:, 0:256].rearrange("p (h t) -> p h t", h=2)),
                             reads=["ps%d" % bq], writes=["of_%d_%d" % (j, par)])
                else:
                    o4 = otmp.rearrange("p (h t) -> p h t", h=4)
                    for par, bq in ((0, oA), (1, oB)):
                        P.op("dve", (lambda par, bq: (lambda e: e.tensor_tensor(o4[:, par::2, :], ps[bq][:, 0:256].rearrange("p (h t) -> p h t", h=2),
                                                                                ofT[:, par::2, j * 128:(j + 1) * 128], ALU.add)))(par, bq),
                             reads=["ps%d" % bq, "of_%d_%d" % (j, par)], writes=["otmp%d" % par])
                    P.op("act", lambda e: e.activation(sqo, otmp, AF.Square), reads=["otmp0", "otmp1"], writes=["sqo"])
                    b4 = self.bank()
                    P.group("pe", [mm(ps[b4], ones_bf, sqo, True, True)], reads=["sqo", "cb"], writes=["ps%d" % b4])
                    P.op("act", lambda e: e.activation(rso, ps[b4], AF.Ln, bias=EPS, scale=8.0), reads=["ps%d" % b4], writes=["rso"])
                    P.op("act", lambda e: e.activation(rso, rso, AF.Exp, scale=-0.5), reads=["rso"], writes=["rso"])
                    P.op("dve", lambda e: e.scalar_tensor_tensor(otmp, otmp, gla_g, rso, ALU.mult, ALU.mult), reads=["otmp0", "otmp1", "rso", "small"], writes=["otmp0", "otmp1"])
                    P.op("dve", lambda e: e.tensor_tensor(qkraw[:, j, :].rearrange("p (h t) -> p h t", h=4), o4,
                                                          sgT[:, :, j * 128:(j + 1) * 128], ALU.mult),
                         reads=["otmp0", "otmp1"] + ["sg_%d_%d" % (h, j // 4) for h in range(4)], writes=["qk_%d" % j])

            stageA(0, order[0])
            for n in range(16):
                if n + 1 < 16:
                    stageA(n + 1, order[n + 1])
                if estop > 3:
                    stageB(n, order[n])
            if estop <= 4:
                break
        if estop <= 4:
            P.adopt(["scr"], qk_n + v_n + sg_n)
            P.adopt(hT_all, of_n)
            P.adopt(self.ss_names, ssn + ["ptmp0", "ptmp1"])
            return
        P.adopt(hT_all, of_n)
        P.adopt(self.ss_names, ssn + ["ptmp0", "ptmp1"])
        if self.cfg.get("nogla"):
            self.wq_get("woe_B_%d" % l)
            if after_tt is not None:
                for tt in range(TT):
                    after_tt(tt)
        else:
            self.out_proj(l, "woe_B_%d" % l,
                          lambda kc, tt: qkraw[:, 4 * tt:4 * tt + 4, kc * 128:(kc + 1) * 128],
                          lambda kc, tt: ["qk_%d" % (4 * tt + i) for i in range(4)], after_tt=after_tt)
        P.adopt(["scr"], qk_n + v_n + sg_n)

    def gla_v2(self, l, jl, glrT, wa2b, qkraw, vtok, sgT, ssn, hT_all, gla_g):
        P = self.P
        ps, hT, cb, cf, small = self.ps, self.hT, self.cb, self.cf, self.small
        ones_bf, ident = self.ones_bf, self.ident_bf
        tri32 = [cf[:, 0:128], cf[:, 128:256]]
        onec = cf[:, 256:257]
        mask2 = cb[:, 512:768].rearrange("p (z t) -> p z t", z=2)
        psb = lambda b: ps[b].bitcast(BF16)
        Sst = hT[:, :, :].rearrange("p k t -> p (k t)").rearrange("p (z j c) -> p z j c", z=2, j=16)
        st_n = ["Sst_%d_%d" % (z, j) for z in range(2) for j in range(16)]
        P.adopt(st_n, hT_all)

        def C(name, off, shape, dt):
            ssn.append(name)
            return self.carve(off, shape, dt)
        el_s = [C("g2_el%d" % i, 5120 + 2048 * i, [128, 512], F32) for i in range(2)]
        Em_s = [C("g2_Em%d" % i, 9216 + 2048 * i, [128, 512], F32) for i in range(2)]
        kt_s = [C("g2_kts%d" % i, 13312 + 1024 * i, [128, 512], BF16) for i in range(2)]
        S_ = [C("g2_S%d" % z, 15360 + 2048 * z, [128, 512], F32) for z in range(2)]
        tS_ = [C("g2_tS%d" % z, 19456 + 2048 * z, [128, 512], F32) for z in range(2)]
        ed_s = [C("g2_ed%d" % i, 23552 + 16 * i, [128, 4], F32) for i in range(2)]
        lb_s = [C("g2_lb%d" % i, 23584 + 1024 * i, [128, 512], BF16) for i in range(2)]
        tribf = [cb[:, 512:640], cb[:, 640:768]]
        passS_names = ["g2_lb0", "g2_lb1", "g2_el0", "g2_el1", "g2_Em0", "g2_Em1", "g2_kts0", "g2_kts1", "g2_S0", "g2_S1", "g2_tS0", "g2_tS1", "g2_ed0", "g2_ed1"]
        old_names = [n for n in ssn if n not in passS_names and n not in ("glrT", "wa2b")]
        P.adopt(passS_names, old_names)
        for z in range(2):
            P.dma("sp", "s0in%d" % z, S_[z], self.d_s0[jl, z], writes=["g2_S%d" % z])
            j0 = 0 if z == 0 else 15
            P.op("act", self.cp("act", Sst[:, z, j0, :], S_[z]), reads=["g2_S%d" % z], writes=["Sst_%d_%d" % (z, j0)])

        def make_S(n):
            r = n % 2
            jz = [n, 15 - n]
            el, Em, kt, ed, lb = el_s[r], Em_s[r], kt_s[r], ed_s[r], lb_s[r]
            eln, Emn, ktn, edn, lbn = "g2_el%d" % r, "g2_Em%d" % r, "g2_kts%d" % r, "g2_ed%d" % r, "g2_lb%d" % r
            st = {}

            def s0():
                b = st["b"] = self.bank()
                P.group("pe", [mm(ps[b][:, z * 256:(z + 1) * 256], glrT[0:33, jz[z] * 128:(jz[z] + 1) * 128], wa2b[0:33, z * 256:(z + 1) * 256], True, True) for z in range(2)],
                        reads=["glrT", "wa2b"], writes=["ps%d" % b])

            def s1():
                b = st["b"]
                P.op("act", lambda e: e.activation(el, ps[b], AF.Exp, scale=-1.0), reads=["ps%d" % b], writes=[eln])
                P.op("act", lambda e: e.activation(lb, el, AF.Ln, bias=1.0), reads=[eln], writes=[lbn])

            def s2():
                b2 = st["b2"] = self.bank()
                b3 = st["b3"] = self.bank()
                fns = [mm(ps[b2][:, z * 256:(z + 1) * 256], tribf[z], lb[:, z * 256:(z + 1) * 256], True, True) for z in range(2)]
                fns += [mm(ps[b3][:, i:i + 1], lb[:, i * 128:(i + 1) * 128], ones_bf[:, 0:1], True, True) for i in range(4)]
                P.group("pe", fns, reads=[lbn, "cb"], writes=["ps%d" % b2, "ps%d" % b3])

            def s3():
                b2, b3 = st["b2"], st["b3"]
                P.op("act", lambda e: e.activation(Em, ps[b2], AF.Exp, scale=1.0 / 16.0), reads=["ps%d" % b2], writes=[Emn])
                P.op("act", lambda e: e.activation(ed, ps[b3][:, 0:4], AF.Exp, scale=-1024.0 / 16.0), reads=["ps%d" % b3], writes=[edn])

            def s4():
                for z in range(2):
                    P.op("dve", lambda e: e.tensor_tensor(kt[:, z * 256:(z + 1) * 256], qkraw[:, jz[z], 256:512], Em[:, z * 256:(z + 1) * 256], ALU.mult),
                         reads=["qk_%d" % jz[z], Emn], writes=[ktn])

            def s5():
                for z in range(2):
                    j = jz[z]
                    bz = st["bz%d" % z] = self.bank()
                    P.group("pe", [mm(ps[bz][:, hp * 256:(hp + 1) * 256], kt[:, z * 256 + hp * 128:z * 256 + (hp + 1) * 128], vtok[:, j, hp * 256:(hp + 1) * 256], True, True) for hp in range(2)],
                            reads=[ktn, "v_%d" % j], writes=["ps%d" % bz])

            def s6():
                for z in range(2):
                    j = jz[z]
                    bz = st["bz%d" % z]
                    Sn, tn = "g2_S%d" % z, "g2_tS%d" % z
                    P.op("dve", lambda e: e.tensor_tensor(tS_[z], ps[bz], S_[z], ALU.add), reads=["ps%d" % bz, Sn], writes=[tn])
                    edb = ed[:, 2 * z:2 * z + 2].unsqueeze(2).broadcast_to([128, 2, 256])
                    t3 = tS_[z].rearrange("p (a c) -> p a c", a=2)
                    seg_end = (j % 2 == 1) if z == 0 else (j % 2 == 0)
                    if seg_end:
                        P.op("dve", lambda e: e.tensor_tensor(t3, t3, edb, ALU.mult), reads=[tn, edn], writes=[tn])
                        P.dma("sp", "stout%d" % z, self.d_st[j // 2, jl, z], tS_[z], reads=[tn], writes=["st_%d_%d_%d" % (j // 2, jl, z)])
                        P.op("dve", lambda e: e.tensor_scalar(S_[z], tS_[z], self.keep, None, ALU.mult), reads=[tn, "small"], writes=[Sn])
                    else:
                        P.op("dve", lambda e: e.tensor_tensor(S_[z].rearrange("p (a c) -> p a c", a=2), t3, edb, ALU.mult), reads=[tn, edn], writes=[Sn])
                    jn = j + 1 if z == 0 else j - 1
                    if 0 <= jn <= 15:
                        P.op("act", self.cp("act", Sst[:, z, jn, :], S_[z]), reads=[Sn], writes=["Sst_%d_%d" % (z, jn)])
            return [s0, s1, s2, s3, s4, s5, s6]

        def pipeline(levels_of, n_items, split):
            items = [levels_of(i) for i in range(n_items)]
            nl = len(items[0])
            for i in range(n_items + 1):
                for k in range(max(split, nl - split)):
                    if i < n_items and k < split:
                        items[i][k]()
                    if i >= 1 and split + k < nl:
                        items[i - 1][split + k]()

        pipeline(make_S, 16, 4)
        X_o = [C("g2o_X%d" % i, 5120 + 2048 * i, [128, 512], F32) for i in range(2)]
        Y_o = [C("g2o_Y%d" % i, 9216 + 2048 * i, [128, 512], F32) for i in range(2)]
        qt_o = [C("g2o_qt%d" % i, 13312 + 1024 * i, [128, 512], BF16) for i in range(2)]
        kt_o = [C("g2o_kt%d" % i, 15360 + 1024 * i, [128, 512], BF16) for i in range(2)]
        qkT_o = [C("g2o_qkT%d" % i, 17408 + 2048 * i, [128, 8, 128], BF16) for i in range(2)]
        attm_o = [C("g2o_attm%d_0" % i, 21504 + 2048 * i, [128, 8, 128], BF16) for i in range(2)]
        ssn.extend(["g2o_attm0_1", "g2o_attm1_1"])
        ssn.extend(["g2o_qkT0k", "g2o_qkT1k"])
        passO_names = ["g2o_X0", "g2o_X1", "g2o_Y0", "g2o_Y1", "g2o_qt0", "g2o_qt1", "g2o_kt0", "g2o_kt1", "g2o_qkT0", "g2o_qkT1", "g2o_qkT0k", "g2o_qkT1k",
                       "g2o_attm0_0", "g2o_attm0_1", "g2o_attm1_0", "g2o_attm1_1"]
        P.adopt(passO_names, passS_names)
        order = list(range(15, -1, -1))

        def make_O(n):
            j = order[n]
            r = n % 2
            X, Y, qt, kt, T_, attm = X_o[r], Y_o[r], qt_o[r], kt_o[r], qkT_o[r], attm_o[r]
            Xn, Yn, qtn, ktn, Tn = "g2o_X%d" % r, "g2o_Y%d" % r, "g2o_qt%d" % r, "g2o_kt%d" % r, "g2o_qkT%d" % r
            amn = ["g2o_attm%d_0" % r, "g2o_attm%d_1" % r]
            st = {}

            def o0():
                b = st["b"] = self.bank()
                P.group("pe", [mm(ps[b], glrT[0:33, j * 128:(j + 1) * 128], wa2b[0:33, :], True, True)], reads=["glrT", "wa2b"], writes=["ps%d" % b])

            def o1():
                b = st["b"]
                P.op("act", lambda e: e.activation(X, ps[b], AF.Exp, scale=-1.0), reads=["ps%d" % b], writes=[Xn])
                P.op("act", lambda e: e.activation(qt, X, AF.Ln, bias=1.0), reads=[Xn], writes=[qtn])

            def o2():
                b2 = st["b2"] = self.bank()
                P.group("pe", [mm(ps[b2][:, z * 256:(z + 1) * 256], tribf[z], qt[:, z * 256:(z + 1) * 256], True, True) for z in range(2)],
                        reads=[qtn, "cb"], writes=["ps%d" % b2])

            def o3():
                b2 = st["b2"]
                P.op("act", lambda e: e.activation(X, ps[b2], AF.Exp, scale=-1.0 / 16.0), reads=["ps%d" % b2], writes=[Xn])
                P.op("act", lambda e: e.activation(Y, ps[b2], AF.Exp, scale=1.0 / 16.0), reads=["ps%d" % b2], writes=[Yn])

            def o4():
                q2 = qkraw[:, j, 0:256].unsqueeze(1).broadcast_to([128, 2, 256])
                k2 = qkraw[:, j, 256:512].unsqueeze(1).broadcast_to([128, 2, 256])
                P.op("dve", lambda e: e.scalar_tensor_tensor(qt.rearrange("p (z c) -> p z c", z=2), q2, 0.125, X.rearrange("p (z c) -> p z c", z=2), ALU.mult, ALU.mult),
                     reads=["qk_%d" % j, Xn], writes=[qtn])
                P.op("dve", lambda e: e.tensor_tensor(kt.rearrange("p (z c) -> p z c", z=2), k2, Y.rearrange("p (z c) -> p z c", z=2), ALU.mult),
                     reads=["qk_%d" % j, Yn], writes=[ktn])

            def o5():
                b3 = st["b3"] = self.bank()
                pb = psb(b3)
                fns = []
                for i8 in range(8):
                    src = (qt if i8 < 4 else kt)[:, (i8 % 4) * 128:(i8 % 4 + 1) * 128]
                    fns.append((lambda i8, src: (lambda e: e.transpose(pb[:, i8 * 128:(i8 + 1) * 128], src, ident)))(i8, src))
                P.group("pe", fns, reads=[qtn, ktn, "cb"], writes=["ps%d" % b3])

            def o6():
                b3 = st["b3"]
                P.op("act", self.cp("act", T_, psb(b3).rearrange("p (a t) -> p a t", a=8)), reads=["ps%d" % b3], writes=[Tn])

            def o7():
                bA, bB = st["bA"], st["bB"] = self.bank(), self.bank()
                fns = []
                for z in range(2):
                    for h in range(4):
                        lo = (h % 2) * 64
                        bq = bA if h % 2 == 0 else bB
                        c0 = (z * 2 + h // 2) * 128
                        fns.append(mm(ps[bq][:, c0:c0 + 128], T_[lo:lo + 64, 4 + z * 2 + h // 2, :], T_[lo:lo + 64, z * 2 + h // 2, :], True, True))
                P.group("pe", fns, reads=[Tn], writes=["ps%d" % bA, "ps%d" % bB])

            def o8():
                am5 = attm.rearrange("p (z hh par) t -> p z hh par t", z=2, hh=2)
                for par, bq in ((0, st["bA"]), (1, st["bB"])):
                    P.op("dve", lambda e: e.tensor_tensor(am5[:, :, :, par, :], ps[bq].rearrange("p (z hh t) -> p z hh t", z=2, hh=2),
                                                          mask2.unsqueeze(2).broadcast_to([128, 2, 2, 128]), ALU.mult),
                         reads=["ps%d" % bq, "cb"], writes=[amn[par]])

            def o9():
                oA, oB = st["oA"], st["oB"] = self.bank(), self.bank()
                fns = []
                for h in range(4):
                    lo = (h % 2) * 64
                    o = ps[oA if h % 2 == 0 else oB][:, (h // 2) * 128:(h // 2 + 1) * 128]
                    sc0 = (h // 2) * 256 + (h % 2) * 128
                    for z in range(2):
                        fns.append(mm(o, vtok[:, j, h * 128:(h + 1) * 128], attm[:, z * 4 + h, :], z == 0, False))
                        fns.append(mm(o, Sst[lo:lo + 64, z, j, sc0:sc0 + 128], T_[lo:lo + 64, z * 2 + h // 2, :], False, z == 1))
                P.group("pe", fns, reads=["v_%d" % j, amn[0], amn[1], "Sst_0_%d" % j, "Sst_1_%d" % j, Tn], writes=["ps%d" % oA, "ps%d" % oB])

            def o10():
                sq4 = qt.rearrange("p (h t) -> p h t", h=4)
                for par, bq in ((0, st["oA"]), (1, st["oB"])):
                    P.op("act", lambda e: e.activation(sq4[:, par::2, :], ps[bq][:, 0:256].rearrange("p (h t) -> p h t", h=2), AF.Square),
                         reads=["ps%d" % bq], writes=[qtn])

            def o11():
                b4 = st["b4"] = self.bank()
                P.group("pe", [mm(ps[b4], ones_bf, qt, True, True)], reads=[qtn, "cb"], writes=["ps%d" % b4])

            def o12():
                b4 = st["b4"]
                P.op("act", lambda e: e.activation(X, ps[b4], AF.Ln, bias=EPS, scale=8.0), reads=["ps%d" % b4], writes=[Xn])
                P.op("act", lambda e: e.activation(X, X, AF.Exp, scale=-0.5), reads=[Xn], writes=[Xn])

            def o13():
                r4 = X.rearrange("p (h t) -> p h t", h=4)
                o4 = Y.rearrange("p (h t) -> p h t", h=4)
                for par, bq in ((0, st["oA"]), (1, st["oB"])):
                    P.op("dve", lambda e: e.scalar_tensor_tensor(o4[:, par::2, :], ps[bq][:, 0:256].rearrange("p (h t) -> p h t", h=2), gla_g, r4[:, par::2, :], ALU.mult, ALU.mult),
                         reads=["ps%d" % bq, Xn, "small"], writes=[Yn])
                P.op("dve", lambda e: e.tensor_tensor(qkraw[:, j, :].rearrange("p (h t) -> p h t", h=4), o4, sgT[:, :, j * 128:(j + 1) * 128], ALU.mult),
                     reads=[Yn] + ["sg_%d_%d" % (h, j // 4) for h in range(4)], writes=["qk_%d" % j])
            return [o0, o1, o2, o3, o4, o5, o6, o7, o8, o9, o10, o11, o12, o13]

        pipeline(make_O, 16, 7)
        P.adopt(hT_all, st_n)

    def out_proj(self, l, wname, src, srcres, after_tt=None):
        P = self.P
        g1 = self.mod_part(l, 2)
        g1dep = self.mod_res(l, 2)
        w, wres = self.wq_get(wname)
        for tt in range(TT):
            sl = slice(tt * TW, (tt + 1) * TW)
            for m in range(KC):
                b = self.bank()
                P.group("pe", [mm(self.ps[b], w[:, kc, m * 128:(m + 1) * 128], src(kc, tt), kc == 0, kc == 3) for kc in range(4)],
                        reads=[wres] + [r for kc in range(4) for r in srcres(kc, tt)], writes=["ps%d" % b])
                xr = "x_%d_%d" % (m, tt)
                P.op("dve", lambda e: e.scalar_tensor_tensor(self.x[:, m, sl], self.ps[b], g1[:, m:m + 1], self.x[:, m, sl], ALU.mult, ALU.add),
                     reads=["ps%d" % b, xr] + g1dep, writes=[xr])
            if after_tt is not None:
                after_tt(tt)

    def mixer_odd(self, l, after_tt=None):
        P = self.P
        jl = l // 2
        scr = self.scr
        ps = self.ps
        hT = self.hT
        fT = scr[:, 0:8192].rearrange("p (g t) -> p g t", g=4)
        AT = scr[:, 8192:24576].rearrange("p (lt g c) -> p lt g c", lt=16, g=4)
        fT_n = ["fT_%d_%d" % (g, tt) for g in range(4) for tt in range(TT)]
        AT_n = ["AT_%d_%d" % (lt, gp) for lt in range(16) for gp in range(2)]
        P.adopt(fT_n + AT_n, ["scr"])
        CS = self.cb[:, 256:512]
        w, wres = self.wq_get("wio_f_%d" % l)
        for tt in range(TT):
            for g in range(4):
                sl = slice(tt * TW, (tt + 1) * TW)
                b = self.bank()
                P.group("pe", [mm(ps[b], w[:, kc, g * 128:(g + 1) * 128], hT[:, kc, sl], kc == 0, kc == KC - 1) for kc in range(KC)],
                        reads=[wres] + ["hT_%d_%d" % (kc, tt) for kc in range(KC)], writes=["ps%d" % b])
                P.op("act", (lambda b, g, sl: (lambda e: e.copy(fT[:, g, sl], ps[b])))(b, g, sl), reads=["ps%d" % b], writes=["fT_%d_%d" % (g, tt)])
        for lt in range(16):
            for gp in range(2):
                b = self.bank()
                P.group("pe", [mm(ps[b][:, gg * 256:(gg + 1) * 256], fT[:, gp * 2 + gg, lt * 128:(lt + 1) * 128], CS, True, True) for gg in range(2)],
                        reads=["cb"] + ["fT_%d_%d" % (gp * 2 + gg, lt // 4) for gg in range(2)], writes=["ps%d" % b])
                P.op("dve" if (lt + gp) % 2 else "act",
                     (lambda b, lt, gp: (lambda e: (e.copy if e is self.nc.scalar else e.tensor_copy)(AT[:, lt, gp * 2:gp * 2 + 2, :], ps[b].rearrange("p (g c) -> p g c", g=2))))(b, lt, gp),
                     reads=["ps%d" % b], writes=["AT_%d_%d" % (lt, gp)])
        fourT = fT
        four_n = ["four_%d_%d" % (g, tq) for g in range(4) for tq in range(TT)]
        P.adopt(four_n, fT_n)
        for tq in range(TT):
            bg = [self.bank() for g in range(4)]
            for pi in range(4):
                t, half = pi // 2, pi % 2
                wp, wpres = self.wq_get("dft_%d_%d" % (l, tq * 4 + pi))
                for g in range(4):
                    fns = [mm(ps[bg[g]], AT[:, half * 8 + i, g, t * 128:(t + 1) * 128], wp[:, i, :], pi == 0 and i == 0, pi == 3 and i == 7) for i in range(8)]
                    P.group("pe", fns, reads=[wpres] + ["AT_%d_%d" % (half * 8 + i, g // 2) for i in range(8)], writes=["ps%d" % bg[g]])
            for g in range(4):
                eng = "act" if g % 2 else "dve"
                P.op(eng, self.cp(eng, fourT[:, g, tq * TW:(tq + 1) * TW], ps[bg[g]]), reads=["ps%d" % bg[g]], writes=["four_%d_%d" % (g, tq)])
        self.out_proj(l, "woo_A_%d" % l, lambda kc, tt: fourT[:, kc, tt * TW:(tt + 1) * TW],
                      lambda kc, tt: ["four_%d_%d" % (kc, tt)])
        ycT = fourT
        yc_n = ["yc_%d" % ci for ci in range(4)]
        P.adopt(yc_n, four_n)
        zp = self.carve(16384, [128, 8, 258], F32, base=scr)
        cbuf = self.carve(16384 + 8448, [128, 8, 256], F32, base=scr)
        bgb = self.carve(16384 + 8448 + 8192, [128, 8, 256], BF16, base=scr)
        P.adopt(["zp", "cbuf", "bgb"], AT_n)
        for ci in range(4):
            wci, wcires = self.wq_get("wio_c_%d_%d" % (l, ci))
            P.op("pool", lambda e: e.memset(zp[:, :, 0:258:257], 0.0), writes=["zp"])
            for tt in range(TT):
                sl = slice(tt * TW, (tt + 1) * TW)
                bx, bb, bc = self.bank(), self.bank(), self.bank()
                for blk, b in ((0, bx), (1, bb), (2, bc)):
                    P.group("pe", [mm(ps[b], wci[:, kc, blk, :], hT[:, kc, sl], kc == 0, kc == KC - 1) for kc in range(KC)],
                            reads=[wcires] + ["hT_%d_%d" % (kc, tt) for kc in range(KC)], writes=["ps%d" % b])
                t = self.tmp[self.i_tmp % 3]
                tn = "tmp%d" % (self.i_tmp % 3)
                self.i_tmp += 1
                P.op("act", (lambda t, bx: (lambda e: e.copy(t, ps[bx])))(t, bx), reads=["ps%d" % bx], writes=[tn])
                P.op("act", (lambda bb, tt: (lambda e: e.copy(bgb[:, 2 * tt:2 * tt + 2, :], ps[bb].rearrange("p (s c) -> p s c", s=2))))(bb, tt),
                     reads=["ps%d" % bb], writes=["bgb"])
                P.op("dve", (lambda t, bc, tt: (lambda e: e.tensor_tensor(zp[:, 2 * tt:2 * tt + 2, 1:257], ps[bc].rearrange("p (s c) -> p s c", s=2),
                                                                          t.rearrange("p (s c) -> p s c", s=2), ALU.mult)))(t, bc, tt),
                     reads=["ps%d" % bc, tn], writes=["zp"])
            P.op("dve", lambda e: e.tensor_scalar(zp[:, 1:8, 0:1], zp[:, 0:7, 256:257], self.keep, None, ALU.mult), reads=["zp", "small"], writes=["zp"])
            P.op("dve", lambda e: e.tensor_scalar(zp[:, 0:7, 257:258], zp[:, 1:8, 1:2], self.keep, None, ALU.mult), reads=["zp", "small"], writes=["zp"])
            cw = lambda k: self.small[:, 96 + (jl * 3 + k) * 4 + ci: 96 + (jl * 3 + k) * 4 + ci + 1]
            cbias = self.small[:, 120 + jl * 4 + ci: 120 + jl * 4 + ci + 1]
            P.op("dve", (lambda ci, cw, cbias: (lambda e: e.tensor_scalar(cbuf, zp[:, :, 1:257], cw(1), cbias, ALU.mult, ALU.add)))(ci, cw, cbias),
                 reads=["zp", "small"], writes=["cbuf"])
            P.op("dve", (lambda ci, cw: (lambda e: e.scalar_tensor_tensor(cbuf, zp[:, :, 0:256], cw(0), cbuf, ALU.mult, ALU.add)))(ci, cw),
                 reads=["zp", "small", "cbuf"], writes=["cbuf"])
            P.op("dve", (lambda ci, cw: (lambda e: e.scalar_tensor_tensor(cbuf, zp[:, :, 2:258], cw(2), cbuf, ALU.mult, ALU.add)))(ci, cw),
                 reads=["zp", "small", "cbuf"], writes=["cbuf"])
            P.op("dve", (lambda ci: (lambda e: e.tensor_tensor(ycT[:, ci, :].rearrange("p (s c) -> p s c", s=8), bgb, cbuf, ALU.mult)))(ci),
                 reads=["bgb", "cbuf"], writes=["yc_%d" % ci])
        self.out_proj(l, "woo_B_%d" % l, lambda kc, tt: ycT[:, kc, tt * TW:(tt + 1) * TW], lambda kc, tt: ["yc_%d" % kc], after_tt=after_tt)
        P.adopt(["scr"], yc_n + ["zp", "cbuf", "bgb"])

    def build(self):
        cfg = self.cfg
        P = self.P
        depth = cfg["depth"]
        self.plan_weights()
        mlp_on = cfg.get("mlp", True)
        mix_on = cfg.get("mixer", True)
        overlap = cfg.get("overlap", True) and mlp_on and mix_on
        self._ada0_deferred = mix_on and cfg.get("even", True)

        def run_all(ctx):
            for tt in range(TT):
                self.norm_tt(ctx, tt)

        def lazy_hook(make_ctx):
            box = {}

            def h(tt):
                if "c" not in box:
                    box["c"] = make_ctx()
                self.norm_tt(box["c"], tt)
            return h

        fin = lambda: self.norm_begin(0, 0, final=True)
        self.prologue(after_tt=lazy_hook(lambda: self.norm_begin(0, 0)) if overlap else None)
        if not self._ada0_deferred:
            self.ada_finish(0)
        if not mlp_on:
            for l2 in range(1, depth):
                self.ada_start(l2)
                self.ada_finish(l2)
        if overlap:
            for l in range(depth):
                last = l + 1 == depth
                self.mixer(l, after_tt=lazy_hook((lambda l: (lambda: self.norm_begin(l, 1)))(l)))
                self.mlp(l, after_tt=lazy_hook(fin if last else (lambda l: (lambda: self.norm_begin(l + 1, 0)))(l)))
        else:
            for l in range(depth):
                if mix_on:
                    run_all(self.norm_begin(l, 0))
                    self.mixer(l)
                if mlp_on:
                    run_all(self.norm_begin(l, 1))
                    self.mlp(l)
            run_all(fin())
        P.finish("sp", ["yT_%d_%d" % (kc, tt) for kc in range(KC) for tt in range(TT)])
        P.finish("sp", [n for n in P.res if n.startswith("st_")])
        return self.nc


def mlp(self, l, after_tt=None):
    P = self.P
    g2 = self.mod_part(l, 5)
    g2dep = self.mod_res(l, 5)
    a = self.scr[:, 0:16384].rearrange("p (c t) -> p c t", c=8)
    an = "aT"
    aT_n = [an + "_%d_%d" % (c, t) for c in range(8) for t in range(TT)]
    P.adopt(aT_n, ["scr"])
    nxt = l + 1 if l + 1 < self.cfg["depth"] else None
    if nxt is not None:
        self.ada_start(nxt)
    for hb in range(4):
        for j in range(2):
            w1, w1res = self.wq_get("w1_%d_%d_%d" % (l, hb, j))
            if nxt is not None and hb < 3:
                self.ada_step(nxt, 2)
            for tt in range(TT):
                sl = slice(tt * TW, (tt + 1) * TW)
                for c4 in range(4):
                    c = j * 4 + c4
                    b = self.bank()
                    P.group("pe", [mm(self.ps[b], w1[:, kc, c4 * 128:(c4 + 1) * 128], self.hT[:, kc, sl], kc == 0, kc == KC - 1) for kc in range(KC)],
                            reads=[w1res] + ["hT_%d_%d" % (kc, tt) for kc in range(KC)], writes=["ps%d" % b])
                    r = self.rl[self.i_rl % 3]
                    rn = "rl%d" % (self.i_rl % 3)
                    self.i_rl += 1
                    P.op("act", lambda e: e.activation(r, self.ps[b], AF.Relu), reads=["ps%d" % b], writes=[rn])
                    P.op("dve", lambda e: e.tensor_tensor(a[:, c, sl], self.ps[b], r, ALU.mult),
                         reads=["ps%d" % b, rn], writes=[an + "_%d_%d" % (c, tt)])
        w2a, w2ares = self.wq_get("w2_%d_%d_0" % (l, hb))
        w2b, w2bres = self.wq_get("w2_%d_%d_1" % (l, hb))
        for tt in range(TT):
            sl = slice(tt * TW, (tt + 1) * TW)
            for m in range(KC):
                b = self.bank()
                fns = []
                for c in range(8):
                    w2 = w2a if c < 4 else w2b
                    fns.append(mm(self.ps[b], w2[:, c % 4, m * 128:(m + 1) * 128], a[:, c, sl], c == 0, c == 7))
                P.group("pe", fns, reads=[w2ares, w2bres] + [an + "_%d_%d" % (c, tt) for c in range(8)], writes=["ps%d" % b])
                xr = "x_%d_%d" % (m, tt)
                P.op("dve", lambda e: e.scalar_tensor_tensor(self.x[:, m, sl], self.ps[b], g2[:, m:m + 1], self.x[:, m, sl], ALU.mult, ALU.add),
                     reads=["ps%d" % b, xr] + g2dep, writes=[xr])
            if hb == 3 and after_tt is not None:
                after_tt(tt)
    if nxt is not None:
        self.ada_finish(nxt)
    P.adopt(["scr"], aT_n)


Builder.mlp = mlp


def _ada_slots(self):
    return [self.scr[:, 16384 + i * 4096:16384 + (i + 1) * 4096].rearrange("p (k n) -> p k n", k=8) for i in range(2)]


def _ada_start(self, l):
    self.P.adopt(["adaslot0", "adaslot1"], ["scr"])
    self._ada_l = l
    self._ada_issued = 0
    self._ada_used = 0
    self._ada_issue()
    self._ada_issue()


def _ada_issue(self):
    i = self._ada_issued
    if i >= 12:
        return
    l = self._ada_l
    s_ = i % 2
    src = self.d_wada[l].rearrange("(kc p) n -> p kc n", p=128)[:, :, i * 512:(i + 1) * 512]
    self.P.dma("pool", "ada%d" % s_, self._ada_slots()[s_], src, writes=["adaslot%d" % s_])
    self._ada_issued += 1


def _ada_step(self, l, n):
    P = self.P
    assert l == self._ada_l
    for _ in range(n):
        i = self._ada_used
        if i >= 12:
            return
        s_ = i % 2
        w = self._ada_slots()[s_]
        b = self.bank()
        fns = []
        for c in range(4):
            for kc in range(KC):
                fns.append(mm(self.ps[b][:, c:c + 1], w[:, kc, c * 128:(c + 1) * 128], self.scb[:, kc:kc + 1], kc == 0, kc == KC - 1))
        P.group("pe", fns, reads=["adaslot%d" % s_, "scb"], writes=["ps%d" % b])
        P.op("dve", lambda e: e.tensor_tensor(self.mod[:, l, i * 4:(i + 1) * 4], self.ps[b][:, 0:4], self.bada[:, l, i * 4:(i + 1) * 4], ALU.add),
             reads=["ps%d" % b, "bada"], writes=["mod_%d_%d" % (l, i)])
        self._ada_used += 1
        self._ada_issue()


def _ada_finish(self, l):
    self.ada_step(l, 12 - self._ada_used)
    self.P.adopt(["scr"], ["adaslot0", "adaslot1"])


Builder._ada_slots = _ada_slots
Builder.ada_start = _ada_start
Builder._ada_issue = _ada_issue
Builder.ada_step = _ada_step
Builder.ada_finish = _ada_finish


_CACHE = {}


def grid_pos_embed_T():
    rows = NT // 64
    r = np.repeat(np.arange(rows, dtype=np.float32), 64)
    col = np.tile(np.arange(64, dtype=np.float32), rows)
    quarter = D // 4
    freqs = (1.0 / (10000.0 ** (np.arange(quarter, dtype=np.float32) / np.float32(quarter)))).astype(np.float32)
    ar = r[:, None] * freqs
    ac = col[:, None] * freqs
    pe = np.concatenate([np.sin(ar), np.cos(ar), np.sin(ac), np.cos(ac)], axis=-1).astype(np.float32)
    return np.ascontiguousarray(pe.T)


def dft_pieces(Ls):
    l = np.arange(NT)
    seg = l // Ls
    k = ((l % Ls)[:, None] * (l % Ls)[None, :]) % Ls
    ang = 2.0 * np.pi * k.astype(np.float64) / Ls
    same = (seg[:, None] == seg[None, :])
    sc = 1.0 / np.sqrt(Ls * 128.0)
    C = np.where(same, np.cos(ang), 0.0) * sc
    S = np.where(same, -np.sin(ang), 0.0) * sc
    out = np.zeros((16, 128, 8, 512), np.float32)
    for tq in range(4):
        for t, M in enumerate((C, S)):
            blk = M[:, tq * 512:(tq + 1) * 512].reshape(16, 128, 512).transpose(1, 0, 2)
            for half in range(2):
                out[tq * 4 + t * 2 + half] = blk[:, half * 8:(half + 1) * 8, :]
    return out.astype(ml_dtypes.bfloat16)


def state_layout(st):
    out = np.zeros((2, 2, 128, 2, 2, 128), np.float32)
    for h in range(4):
        out[:, :, (h % 2) * 64:(h % 2) * 64 + 64, h // 2, h % 2, :] = st[:, :, h]
    return out.reshape(2, 2, 128, 512)


def state_unlayout(a):
    a = a.reshape(a.shape[:-2] + (128, 2, 2, 128))
    out = np.zeros(a.shape[:-4] + (4, 64, 128), np.float32)
    for h in range(4):
        out[..., h, :, :] = a[..., (h % 2) * 64:(h % 2) * 64 + 64, h // 2, h % 2, :]
    return out


def band_library(Ls):
    out = np.zeros((128, 4, 8, 128), np.float32)
    t = np.arange(NT)
    seg0 = (t // Ls) * Ls
    for g, w in enumerate((2, 4, 8, 16)):
        lo = np.clip(t - w // 2, seg0, seg0 + Ls)
        hi = np.clip(t + w - w // 2, seg0, seg0 + Ls)
        cnt = (hi - lo).astype(np.float64)
        Pm = np.zeros((NT, NT), np.float64)
        for d in range(NT):
            Pm[lo[d]:hi[d], d] = 1.0 / cnt[d]
            Pm[d, d] -= 1.0
        blk = lambda a, b: Pm[a * 128:(a + 1) * 128, b * 128:(b + 1) * 128]
        out[:, g, 0] = blk(0, 0)
        out[:, g, 1] = blk(2, 2)
        out[:, g, 2] = blk(1, 1)
        out[:, g, 3] = blk(15, 15)
        out[:, g, 4] = blk(0, 1)
        out[:, g, 5] = blk(1, 2)
        out[:, g, 6] = blk(1, 0)
        out[:, g, 7] = blk(2, 1)
    return out.reshape(128, 4096).astype(ml_dtypes.bfloat16)


def host_consts():
    cb = np.zeros((128, 768), np.float32)
    tri_f = (np.arange(128)[:, None] <= np.arange(128)[None, :]).astype(np.float32)
    tri_b = (np.arange(128)[:, None] >= np.arange(128)[None, :]).astype(np.float32)
    cb[:, 512:640] = tri_f
    cb[:, 640:768] = tri_b
    cf = np.zeros((128, 260), np.float32)
    cf[:, 0:128] = tri_f
    cf[:, 128:256] = tri_b
    cf[:, 256:260] = 1.0
    cb[:, 0:128] = 1.0 / 1024.0
    cb[:, 128:256] = np.eye(128, dtype=np.float32)
    c = np.arange(128)
    ang = 2.0 * np.pi * ((c[:, None] * c[None, :]) % 128).astype(np.float64) / 128.0
    cb[:, 256:384] = np.cos(ang)
    cb[:, 384:512] = np.sin(ang)
    return {"cbf": cb.astype(ml_dtypes.bfloat16), "cf32": cf}


def make_in_maps(inp, cfg):
    f = lambda a: np.ascontiguousarray(np.asarray(a, dtype=np.float32))
    shared = {
        "b_ada": f(np.asarray(inp["b_ada"]).reshape(DEPTH, 48, 128).transpose(0, 2, 1)),
        "w_ada": f(inp["w_ada"]), "w_mlp1": f(inp["w_mlp1"]), "w_mlp2": f(inp["w_mlp2"]),
        "w_in_odd": f(inp["w_in_odd"]), "w_out_odd": f(inp["w_out_odd"]),
        "w_in_even": f(inp["w_in_even"]), "w_out_even": f(inp["w_out_even"]),
        "w_a2": f(inp["w_a2"]), "b_a2": f(np.asarray(inp["b_a2"]).reshape(2, 512)), "pool_w": f(inp["pool_w"]),
    }
    band_s = band_library(2048)
    band_p = band_library(256)
    psl = np.asarray(inp["pool_s"], np.float32).reshape(2, 4, 128).transpose(2, 0, 1).reshape(128, 8)
    glg = np.asarray(inp["gla_norm_g"], np.float32).T
    sg = np.asarray(inp["state_gla"], np.float32)
    dft_s = dft_pieces(2048)
    dft_p = dft_pieces(256)
    cw = np.asarray(inp["conv_w"], np.float32).reshape(2, 3, 4, 128).transpose(3, 0, 1, 2).reshape(128, 24)
    cbi = np.asarray(inp["conv_b"], np.float32).reshape(2, 4, 128).transpose(2, 0, 1).reshape(128, 8)
    shared.update(host_consts())
    ng = np.asarray(inp["norm_g"], np.float32).reshape(DEPTH, 2, KC, 128).transpose(3, 0, 1, 2).reshape(128, 64)
    fg = np.asarray(inp["final_g"], np.float32).reshape(KC, 128).T
    peT = grid_pos_embed_T()
    peRC = np.zeros((128, 384), np.float32)
    peRC[:, 0:128] = peT[0:512, 0::64].reshape(4, 128, 32).transpose(1, 0, 2).reshape(128, 128)
    peRC[:, 128:384] = peT[512:1024, 0:64].reshape(4, 128, 64).transpose(1, 0, 2).reshape(128, 256)
    zpe = np.zeros_like(peRC)
    xs = np.asarray(inp["x_sample"], np.float32)
    xp = np.asarray(inp["x_prompt"], np.float32)
    maps = []
    dummy = None
    for core in cfg.get("cores", PHYS):
        if core == 6:
            if dummy is None:
                dummy = {k: (v if k in ("cbf", "cf32") else np.zeros_like(v)) for k, v in maps[0].items()}
            maps.append(dummy)
            continue
        m = dict(shared)
        small = np.zeros((128, 192), np.float32)
        small[:, 16:80] = ng
        small[:, 80:88] = fg
        small[:, 96:120] = cw
        small[:, 120:128] = cbi
        small[:, 128:136] = psl
        small[:, 136:138] = glg
        if core < 4:
            xT = xs[core].T
            cond = np.asarray(inp["c"], np.float32)[core]
            small[:, 8] = 1.0
            m["peRC"] = peRC
            m["dftm"] = dft_s
            m["bandm"] = band_s
            m["s0"] = state_layout(sg[core])
        else:
            g = (core - 4) % 2
            xT = xp[g * 8:(g + 1) * 8].reshape(NT, D).T
            cond = np.asarray(inp["c_ctx"], np.float32)
            small[:, 8] = 0.0
            m["peRC"] = zpe
            m["dftm"] = dft_p
            m["bandm"] = band_p
            m["s0"] = np.zeros((2, 2, 128, 512), np.float32)
        small[:, 0:8] = cond.reshape(KC, 128).T
        m["xT"] = np.ascontiguousarray(xT)
        m["small"] = small
        maps.append(m)
    return maps


def get_nc(cfg):
    key = tuple(sorted((k, v) for k, v in cfg.items() if k != "cores"))
    if key not in _CACHE:
        _CACHE[key] = Builder(dict(cfg)).build()
    return _CACHE[key]


def run(inp, cfg, trace=False):
    nc = get_nc(cfg)
    maps = make_in_maps(inp, cfg)
    res = run_bass_kernel_spmd(nc, maps, core_ids=list(range(len(maps))), **({"trace": True} if trace else {}))
    return res


PHYS = (0, 1, 4, 6, 2, 3, 5, 6)


def assemble(res):
    B, S = 16, 256
    y_prompt = np.zeros((B, S, D), np.float32)
    y_sample = np.zeros((4, NT, D), np.float32)
    for b in range(4):
        y_sample[b] = res.results[PHYS.index(b)]["yT"].T
    for g in range(2):
        y_prompt[g * 8:(g + 1) * 8] = res.results[PHYS.index(4 + g)]["yT"].T.reshape(8, S, D)
    return y_prompt, y_sample


CFG = {"depth": 4, "mixer": True, "mlp": True}


def kernel(**inputs):
    res = run(inputs, CFG)
    y_prompt, y_sample = assemble(res)
    st = np.zeros((16, 2, 2, 4, 64, 128), np.float32)
    for g in range(2):
        st[g * 8:(g + 1) * 8] = state_unlayout(res.results[PHYS.index(4 + g)]["st"])
    return (y_prompt, y_sample, st)
```
